# Optimizing a Trainium2 kernel written in Bass

```python
import jax, jax.numpy as jnp
from jax import lax
import numpy as np


D_MODEL = 1024
BATCH = 8
SEQ = 8192
DEPTH = 1

N_Q_HEADS = 16
N_KV_HEADS = 4
HEAD_DIM = 64
WINDOW = 128
ATTN_BLOCK = 128
ROPE_THETA = 500000.0
ROPE_DIM = HEAD_DIM // 4
HGRN_HEADS = 8
HGRN_DK = 128
HGRN_DV = 128
HGRN_CHUNK = 64
D_FF = 2816
ATTN_WIDTH = N_Q_HEADS * HEAD_DIM
KV_WIDTH = N_KV_HEADS * HEAD_DIM
HGRN_KWIDTH = HGRN_HEADS * HGRN_DK
HGRN_VWIDTH = HGRN_HEADS * HGRN_DV
IN_WIDTHS = (ATTN_WIDTH, KV_WIDTH, KV_WIDTH, HGRN_KWIDTH, HGRN_KWIDTH, HGRN_VWIDTH, HGRN_VWIDTH, D_MODEL, D_MODEL)
D_IN = sum(IN_WIDTHS)
DEEPNORM_ALPHA = (2 * DEPTH) ** 0.25
DEEPNORM_BETA = (8 * DEPTH) ** -0.25
LN_EPS = 1e-5
RMS_EPS = 1e-6
NEG_INF = -1e30

kernel_name = 'hybrid_swa_sink_hgrn2_macaron_deepnorm'


def layer_norm(x, g, b):
    xf = x.astype(jnp.float32)
    mu = jnp.mean(xf, axis=-1, keepdims=True)
    var = jnp.mean(jnp.square(xf - mu), axis=-1, keepdims=True)
    y = (xf - mu) * lax.rsqrt(var + LN_EPS) * g.astype(jnp.float32) + b.astype(jnp.float32)
    return y.astype(x.dtype)


def swiglu(x, w1, w3, w2):
    return (jax.nn.silu(x @ w1) * (x @ w3)) @ w2


def rope_tables(seq_len):
    pos = jnp.arange(seq_len, dtype=jnp.float32)
    inv_freq = ROPE_THETA ** (-jnp.arange(0, ROPE_DIM, 2, dtype=jnp.float32) / ROPE_DIM)
    ang = pos[:, None] * inv_freq[None, :]
    return jnp.cos(ang)[None, :, None, :], jnp.sin(ang)[None, :, None, :]


def partial_rope(t, cos, sin):
    tf = t.astype(jnp.float32)
    half = ROPE_DIM // 2
    t1, t2, rest = tf[..., :half], tf[..., half:ROPE_DIM], tf[..., ROPE_DIM:]
    rot = jnp.concatenate([t1 * cos - t2 * sin, t2 * cos + t1 * sin, rest], axis=-1)
    return rot.astype(t.dtype)


def sliding_window_attention(q, k, v, sinks):
    B, S = q.shape[0], q.shape[1]
    nb = S // ATTN_BLOCK
    grp = N_Q_HEADS // N_KV_HEADS
    qb = q.reshape(B, nb, ATTN_BLOCK, N_KV_HEADS, grp, HEAD_DIM)

    def band(t):
        tp = jnp.pad(t, ((0, 0), (ATTN_BLOCK, 0), (0, 0), (0, 0)))
        tp = tp.reshape(B, nb + 1, ATTN_BLOCK, N_KV_HEADS, HEAD_DIM)
        return jnp.concatenate([tp[:, :-1], tp[:, 1:]], axis=2)

    kb, vb = band(k), band(v)
    scores = jnp.einsum('bnqkgd,bnskd->bnkgqs', qb, kb).astype(jnp.float32) * (HEAD_DIM ** -0.5)
    qi = jnp.arange(ATTN_BLOCK)[:, None]
    kj = jnp.arange(2 * ATTN_BLOCK)[None, :]
    blk = jnp.arange(nb)[:, None, None]
    dist = qi + ATTN_BLOCK - kj
    mask = (dist >= 0) & (dist < WINDOW) & (blk * ATTN_BLOCK + kj - ATTN_BLOCK >= 0)
    scores = jnp.where(mask[None, :, None, None], scores, NEG_INF)
    sink = sinks.astype(jnp.float32).reshape(1, 1, N_KV_HEADS, grp, 1, 1)
    m = jnp.maximum(jnp.max(scores, axis=-1, keepdims=True), sink)
    p = jnp.exp(scores - m)
    denom = jnp.sum(p, axis=-1, keepdims=True) + jnp.exp(sink - m)
    probs = (p / denom).astype(v.dtype)
    out = jnp.einsum('bnkgqs,bnskd->bnqkgd', probs, vb)
    return out.reshape(B, S, ATTN_WIDTH)


def hgrn2_recurrence(q, f_logit, v, lb):
    B, S = q.shape[0], q.shape[1]
    nc = S // HGRN_CHUNK
    lb = lb.reshape(HGRN_HEADS, HGRN_DK)
    f = lb + (1.0 - lb) * jax.nn.sigmoid(f_logit.astype(jnp.float32))
    k = 1.0 - f

    def chunks(t):
        return t.reshape(B, nc, HGRN_CHUNK, HGRN_HEADS, t.shape[-1]).transpose(0, 3, 1, 2, 4)

    qc, kc, vc = chunks(q.astype(jnp.float32)), chunks(k), chunks(v.astype(jnp.float32))
    gc = jnp.cumsum(chunks(jnp.log(f)), axis=3)
    g_last = gc[:, :, :, -1:, :]
    q_dec = qc * jnp.exp(gc)
    k_inv = kc * jnp.exp(-gc)
    k_end = kc * jnp.exp(g_last - gc)
    causal = jnp.tril(jnp.ones((HGRN_CHUNK, HGRN_CHUNK), dtype=bool))
    scores = jnp.where(causal, jnp.einsum('bhntd,bhnsd->bhnts', q_dec, k_inv), 0.0)
    o_intra = jnp.einsum('bhnts,bhnse->bhnte', scores, vc)
    upd = jnp.einsum('bhnsd,bhnse->bhnde', k_end, vc)
    decay = jnp.exp(g_last[:, :, :, 0, :])

    def step(state, inp):
        a_n, u_n = inp
        return state * a_n[..., None] + u_n, state

    s0 = jnp.zeros((B, HGRN_HEADS, HGRN_DK, HGRN_DV), jnp.float32)
    _, s_start = lax.scan(step, s0, (jnp.moveaxis(decay, 2, 0), jnp.moveaxis(upd, 2, 0)))
    s_start = jnp.moveaxis(s_start, 0, 2)
    o = o_intra + jnp.einsum('bhntd,bhnde->bhnte', q_dec, s_start)
    return o.transpose(0, 2, 3, 1, 4).reshape(B, S, HGRN_HEADS, HGRN_DV)


def token_mixer(h, w_in, b_in, sinks, lb, norm_g, w_pa, w_ph, w_out, cos, sin):
    B, S = h.shape[0], h.shape[1]
    proj = h @ w_in + b_in
    splits = np.cumsum(IN_WIDTHS)[:-1].tolist()
    q_a, k_a, v_a, f_h, q_h, i_h, og_h, gate_a, gate_h = jnp.split(proj, splits, axis=-1)
    q_a = partial_rope(q_a.reshape(B, S, N_Q_HEADS, HEAD_DIM), cos, sin)
    k_a = partial_rope(k_a.reshape(B, S, N_KV_HEADS, HEAD_DIM), cos, sin)
    v_a = v_a.reshape(B, S, N_KV_HEADS, HEAD_DIM)
    y_attn = sliding_window_attention(q_a, k_a, v_a, sinks)
    q_h = jax.nn.silu(q_h).reshape(B, S, HGRN_HEADS, HGRN_DK)
    o_h = hgrn2_recurrence(q_h, f_h.reshape(B, S, HGRN_HEADS, HGRN_DK),
                           i_h.reshape(B, S, HGRN_HEADS, HGRN_DV), lb)
    o_h = o_h * lax.rsqrt(jnp.mean(jnp.square(o_h), axis=-1, keepdims=True) + RMS_EPS) * norm_g.astype(jnp.float32)
    y_hgrn = o_h.reshape(B, S, HGRN_VWIDTH).astype(h.dtype) * jax.nn.silu(og_h)
    merged = jax.nn.sigmoid(gate_a) * (y_attn @ w_pa) + jax.nn.sigmoid(gate_h) * (y_hgrn @ w_ph)
    return merged @ w_out


def setup_inputs(seed: int = 0) -> dict:
    key = jax.random.key(seed)
    ks = jax.random.split(key, 24)

    def nrm(k, shape, scale):
        return jax.random.normal(k, shape, jnp.float32) * scale

    D = D_MODEL
    return {
        'x': nrm(ks[0], (BATCH, SEQ, D), 1.0),
        'ln1_g': 1.0 + nrm(ks[1], (DEPTH, D), 0.02),
        'ln1_b': nrm(ks[2], (DEPTH, D), 0.02),
        'ffn1_w1': nrm(ks[3], (DEPTH, D, D_FF), D ** -0.5),
        'ffn1_w3': nrm(ks[4], (DEPTH, D, D_FF), D ** -0.5),
        'ffn1_w2': nrm(ks[5], (DEPTH, D_FF, D), D_FF ** -0.5 * DEEPNORM_BETA),
        'ln2_g': 1.0 + nrm(ks[6], (DEPTH, D), 0.02),
        'ln2_b': nrm(ks[7], (DEPTH, D), 0.02),
        'w_in': nrm(ks[8], (DEPTH, D, D_IN), D ** -0.5),
        'b_in': nrm(ks[9], (DEPTH, D_IN), 0.02),
        'attn_sinks': nrm(ks[10], (DEPTH, N_Q_HEADS), 0.5),
        'hgrn_lb_logits': nrm(ks[11], (DEPTH + 1, HGRN_KWIDTH), 0.1),
        'hgrn_norm_g': 1.0 + nrm(ks[12], (DEPTH, HGRN_DV), 0.02),
        'w_proj_attn': nrm(ks[13], (DEPTH, ATTN_WIDTH, D), ATTN_WIDTH ** -0.5 * DEEPNORM_BETA),
        'w_proj_hgrn': nrm(ks[14], (DEPTH, HGRN_VWIDTH, D), HGRN_VWIDTH ** -0.5 * DEEPNORM_BETA),
        'w_out': nrm(ks[15], (DEPTH, D, D), D ** -0.5 * DEEPNORM_BETA),
        'ln3_g': 1.0 + nrm(ks[16], (DEPTH, D), 0.02),
        'ln3_b': nrm(ks[17], (DEPTH, D), 0.02),
        'ffn2_w1': nrm(ks[18], (DEPTH, D, D_FF), D ** -0.5),
        'ffn2_w3': nrm(ks[19], (DEPTH, D, D_FF), D ** -0.5),
        'ffn2_w2': nrm(ks[20], (DEPTH, D_FF, D), D_FF ** -0.5 * DEEPNORM_BETA),
    }


def reference(x, ln1_g, ln1_b, ffn1_w1, ffn1_w3, ffn1_w2, ln2_g, ln2_b, w_in, b_in,
              attn_sinks, hgrn_lb_logits, hgrn_norm_g, w_proj_attn, w_proj_hgrn, w_out,
              ln3_g, ln3_b, ffn2_w1, ffn2_w3, ffn2_w2):
    cos, sin = rope_tables(x.shape[1])
    lb_all = jnp.cumsum(jax.nn.softmax(hgrn_lb_logits.astype(jnp.float32), axis=0), axis=0)
    for l in range(DEPTH):
        x = layer_norm(DEEPNORM_ALPHA * x + 0.5 * swiglu(x, ffn1_w1[l], ffn1_w3[l], ffn1_w2[l]),
                       ln1_g[l], ln1_b[l])
        mix = token_mixer(x, w_in[l], b_in[l], attn_sinks[l], lb_all[l], hgrn_norm_g[l],
                          w_proj_attn[l], w_proj_hgrn[l], w_out[l], cos, sin)
        x = layer_norm(DEEPNORM_ALPHA * x + mix, ln2_g[l], ln2_b[l])
        x = layer_norm(DEEPNORM_ALPHA * x + 0.5 * swiglu(x, ffn2_w1[l], ffn2_w3[l], ffn2_w2[l]),
                       ln3_g[l], ln3_b[l])
    return x
```

```python
import contextlib
import os
import numpy as np
import concourse.bass as bass
import concourse.mybir as mybir
from concourse.bass_utils import run_bass_kernel_spmd

F32 = mybir.dt.float32
BF16 = mybir.dt.bfloat16
AF = mybir.ActivationFunctionType
ALU = mybir.AluOpType

ENGS = ("pe", "act", "dve", "pool", "sp")
SAME_ENGINE_SYNC = {"pe": False, "act": True, "dve": True, "pool": True, "sp": False}


class Op:
    __slots__ = ("eng", "fn", "deps", "signal", "tok", "dma_sem", "idx")

    def __init__(self, eng, fn, dma_sem=None):
        self.eng = eng
        self.fn = fn
        self.deps = []
        self.signal = False
        self.tok = None
        self.dma_sem = dma_sem


class Res:
    __slots__ = ("w", "readers")

    def __init__(self):
        self.w = None
        self.readers = {}


class Prog:
    def __init__(self):
        self.ops = {e: [] for e in ENGS}
        self.res = {}
        self.nops = 0

    def _need(self, d, o):
        if d.dma_sem is not None:
            return True
        if d.eng != o.eng:
            return True
        return SAME_ENGINE_SYNC[o.eng]

    def op(self, eng, fn, reads=(), writes=(), dma_sem=None):
        o = Op(eng, fn, dma_sem)
        o.idx = self.nops
        self.nops += 1
        deps = {}
        for r in reads:
            st = self.res.get(r)
            if st is not None and st.w is not None:
                deps[id(st.w)] = st.w
        for w in writes:
            st = self.res.get(w)
            if st is not None:
                if st.w is not None:
                    deps[id(st.w)] = st.w
                for rd in st.readers.values():
                    deps[id(rd)] = rd
        for d in deps.values():
            if d is not o and self._need(d, o):
                o.deps.append(d)
                d.signal = True
        for r in reads:
            st = self.res.get(r)
            if st is None:
                st = self.res[r] = Res()
            key = o.eng if dma_sem is None else ("dma", o.idx)
            st.readers[key] = o
        for w in writes:
            st = self.res.get(w)
            if st is None:
                st = self.res[w] = Res()
            st.w = o
            st.readers = {}
        self.ops[eng].append(o)
        return o

    def finalize(self, eng_sems):
        dma_counts = {}
        for e in ENGS:
            c = 0
            for o in self.ops[e]:
                if o.dma_sem is not None:
                    k = id(o.dma_sem)
                    dma_counts[k] = dma_counts.get(k, 0) + 16
                    o.tok = (o.dma_sem, dma_counts[k])
                elif o.signal:
                    c += 1
                    o.tok = (eng_sems[e], c)

    def emit(self, eng_name, eng):
        waited = {}
        for o in self.ops[eng_name]:
            need = {}
            for d in o.deps:
                sem, val = d.tok
                k = id(sem)
                if waited.get(k, 0) < val and need.get(k, (None, 0))[1] < val:
                    need[k] = (sem, val)
            for k, (sem, val) in need.items():
                eng.wait_ge(sem, val)
                waited[k] = val
            ins = o.fn(eng)
            if o.dma_sem is not None:
                ins.then_inc(o.dma_sem, 16)
            elif o.signal:
                ins.then_inc(o.tok[0], 1)


D = 1024
KC = 8
DFF = 2816
FC = 22
T = 512
TB = 4
NS = 4
DEBUG_STAGE = 0
ALPHA = 2.0 ** 0.25
C_QA, C_KA, C_VA, C_FH, C_QH, C_IH, C_OG, C_GA, C_GH = 0, 1024, 1280, 1536, 2560, 3584, 4608, 5632, 6656
WNAMES = ["ln1_g", "ln1_b", "ffn1_w1", "ffn1_w3", "ffn1_w2", "ln2_g", "ln2_b", "w_in", "b_in",
          "attn_sinks", "hgrn_lb_logits", "hgrn_norm_g", "w_proj_attn", "w_proj_hgrn", "w_out",
          "ln3_g", "ln3_b", "ffn2_w1", "ffn2_w3", "ffn2_w2"]
WSHAPES = {"ln1_g": [1, D], "ln1_b": [1, D], "ffn1_w1": [D, DFF], "ffn1_w3": [D, DFF], "ffn1_w2": [DFF, D],
           "ln2_g": [1, D], "ln2_b": [1, D], "w_in": [D, 7680], "b_in": [1, 7680], "attn_sinks": [1, 16],
           "hgrn_lb_logits": [2, 1024], "hgrn_norm_g": [1, 128], "w_proj_attn": [D, D], "w_proj_hgrn": [D, D],
           "w_out": [D, D], "ln3_g": [1, D], "ln3_b": [1, D], "ffn2_w1": [D, DFF], "ffn2_w3": [D, DFF],
           "ffn2_w2": [DFF, D]}


def mk_units(W):
    U = {}

    def w13(w1, w3):
        return [[(w1, 0, 8, [(g * 256, 256)]), (w3, 0, 8, [(g * 256, 256)])] for g in range(11)]

    def w2u(w2):
        us = []
        for hf in range(2):
            for (k0, k1) in ((0, 8), (8, 16), (16, 22)):
                pcs = []
                k = k0
                while k < k1:
                    ke = min(k + 4, k1)
                    pcs.append((w2, k, ke, [(hf * 512, 512)]))
                    k = ke
                us.append(pcs)
        return us

    def colu(w, c0):
        return [(w, 0, 4, [(c0, 512)]), (w, 4, 8, [(c0, 512)])]

    U["f1a"] = w13(W["ffn1_w1"], W["ffn1_w3"])
    U["f1b"] = w2u(W["ffn1_w2"])
    U["f2a"] = w13(W["ffn2_w1"], W["ffn2_w3"])
    U["f2b"] = w2u(W["ffn2_w2"])
    win = W["w_in"]
    U["tm"] = [colu(win, C_QA), colu(win, C_QA + 512), colu(win, C_KA), colu(win, C_IH), colu(win, C_IH + 512)]
    U["hg"] = [[(win, ka, ka + 4, [(C_FH + 128 * h, 128), (C_QH + 128 * h, 128), (C_OG + 128 * h, 128)])
                for ka in (0, 4)] for h in range(8)]
    U["g"] = [colu(win, C_GA), colu(win, C_GA + 512), colu(win, C_GH), colu(win, C_GH + 512)]
    U["p"] = [colu(W["w_proj_attn"], 0), colu(W["w_proj_attn"], 512),
              colu(W["w_proj_hgrn"], 0), colu(W["w_proj_hgrn"], 512)]
    U["o"] = [colu(W["w_out"], 0), colu(W["w_out"], 512)]
    return U


def piece_elems(pc):
    return (pc[2] - pc[1]) * sum(w for _, w in pc[3])


KIND_ORDER = ["f1a", "f1b", "tm", "hg", "g", "p", "o", "f2a", "f2b"]
TILE_SEQ = ([("f1a", g) for g in range(11)] + [("f1b", u) for u in range(6)] +
            [("tm", u) for u in range(5)] + [("hg", h) for h in range(8)] +
            [("g", 0), ("p", 0), ("g", 2), ("p", 2), ("g", 1), ("p", 1), ("g", 3), ("p", 3)] +
            [("o", 0), ("o", 1)] +
            [("f2a", g) for g in range(11)] + [("f2b", u) for u in range(6)])


def build(S):
    NT = S // T
    NB = S // 128
    nc = bass.Bass("TRN2", target_bir_lowering=False)
    x = nc.dram_tensor("x", [S, D], F32, kind="ExternalInput").ap()
    rope = nc.dram_tensor("rope_cs", [S, 128], F32, kind="ExternalInput").ap()
    W = {n: nc.dram_tensor(n, WSHAPES[n], F32, kind="ExternalInput").ap() for n in WNAMES}
    out = nc.dram_tensor("out", [S, D], F32, kind="ExternalOutput").ap()
    U = mk_units(W)
    UE = {k: [sum(piece_elems(pc) for pc in u) for u in us] for k, us in U.items()}
    scr = {k: nc.dram_tensor("scr_" + k, [len(U[k]), 128, max(UE[k])], BF16, kind="Internal").ap() for k in U}

    P = Prog()
    es = contextlib.ExitStack()
    with es:
        def sb(name, shape, dt):
            return es.enter_context(nc.sbuf_tensor(name, shape, dt))

        def sem(name):
            return es.enter_context(nc.semaphore(name))

        xa = sb("xa", [128, TB * D], F32)
        xa3 = xa[:].rearrange("p (j n) -> p j n", n=D)
        xT = sb("xT", [128, KC * T], BF16)
        xT3 = xT[:].rearrange("p (c t) -> p c t", t=T)
        gT = sb("gT", [128, FC * T], BF16)
        gT3 = gT[:].rearrange("p (c t) -> p c t", t=T)
        yaT3 = gT[:, 0:8 * T].rearrange("p (c t) -> p c t", t=T)
        yhT3 = gT[:, 8 * T:16 * T].rearrange("p (c t) -> p c t", t=T)
        stg16 = [gT[:, 0:2048], gT[:, 2048:4096]]
        ws = [sb("ws%d" % s, [128, 4096], BF16) for s in range(NS)]
        gbt = sb("gbt", [128, 2 * D], F32)
        gb3 = gbt[:].rearrange("p (a n) -> p a n", n=D)
        sl = [sb("sl%d" % k, [128, T], F32) for k in range(2)]
        st = [sb("st%d" % k, [128, 12], F32) for k in range(4)]
        mv = [sb("mv%d" % k, [128, 2], F32) for k in range(4)]
        sd = [sb("sd%d" % k, [128, 1], F32) for k in range(4)]
        rstd = [sb("rstd%d" % k, [128, 1], F32) for k in range(4)]
        nmr = [sb("nmr%d" % k, [128, 1], F32) for k in range(4)]
        ident = sb("ident", [128, 128], F32)
        identb = sb("identb", [128, 128], BF16)
        ones = sb("ones", [128, 128], BF16)
        onesm = sb("onesm", [128, 128], BF16)
        maskA = sb("maskA", [128, 256], BF16)
        mask2 = sb("mask2", [128, 128], BF16)
        rm = sb("rm", [128, T], F32)
        epsc = sb("epsc", [128, 3], F32)
        bfm = sb("bfm", [128, 40], F32)
        gcol = sb("gcol", [128, 16], F32)
        bcol = sb("bcol", [128, 16], F32)
        lbl = sb("lbl", [128, 16], F32)
        lb = sb("lb", [128, 8], F32)
        oml = sb("oml", [128, 8], F32)
        noml = sb("noml", [128, 8], F32)
        ng = sb("ng", [128, 1], F32)
        esk = sb("esk", [128, 8], F32)
        brow = sb("brow", [1, 2560], BF16)
        cst = sb("cst", [128, TB * 128], F32)
        cst3 = cst[:].rearrange("p (j c) -> p j c", c=128)
        qkb = sb("qkb", [128, TB * 1536], BF16)
        qkb3 = qkb[:].rearrange("p (j n) -> p j n", n=1536)
        qT = sb("qT", [128, 8 * T], BF16)
        qT3 = qT[:].rearrange("p (c t) -> p c t", t=T)
        kT = sb("kT", [128, 4 * 640], BF16)
        kT3 = kT[:].rearrange("p (g t) -> p g t", t=640)
        vt = sb("vt", [128, 5 * 256], BF16)
        vt3 = vt[:].rearrange("p (b n) -> p b n", n=256)
        iht = sb("iht", [128, TB * 1024], BF16)
        iht3 = iht[:].rearrange("p (j n) -> p j n", n=1024)
        rt = sb("rt", [128, 4 * 64], F32)
        rq = sb("rq", [128, 128], F32)
        rt4 = rt[:].rearrange("p (a h d) -> p a h d", h=8, d=8)
        pexp = [sb("pexp%d" % k, [128, 512], BF16) for k in range(2)]
        PT = [sb("PT%d" % k, [128, 512], BF16) for k in range(2)]
        rec = [sb("rec%d" % k, [128, 128], F32) for k in range(2)]
        tmp8a = sb("tmp8a", [128, 2048], F32)
        tmp8b = sb("tmp8b", [128, 2048], F32)
        stg32 = [tmp8a, tmp8b]
        hA, hB, hC, hD = (tmp8a[:, k * T:(k + 1) * T] for k in range(4))
        m1 = tmp8b
        xb = [tmp8a[:, 0:D], tmp8a[:, D:2 * D], tmp8b[:, 0:D], tmp8b[:, D:2 * D]]
        XB_ALIAS = [["hA", "hB"], ["hC", "hD"], [("m1", 0), ("m1", 1)], [("m1", 2), ("m1", 3)]]
        m13 = m1[:].rearrange("p (c t) -> p c t", t=T)
        dec = [sb("dec%d" % k, [128, 8], F32) for k in range(3)]
        qdec = [sb("qdec%d" % k, [128, T], BF16) for k in range(3)]
        kinv = [sb("kinv%d" % k, [128, T], BF16) for k in range(3)]
        kend = [sb("kend%d" % k, [128, T], BF16) for k in range(3)]
        qTf = qT[:].bitcast(F32)
        sog = [qTf[:, 0:T], qTf[:, T:2 * T], qTf[:, 3 * T:4 * T]]
        kendTM = [sb("kendTM%d" % k, [128, T], BF16) for k in range(2)]
        msk = [sb("msk%d" % k, [128, T], BF16) for k in range(2)]
        S16 = [sb("S16_%d" % k, [128, 8 * 128], BF16) for k in range(2)]
        state = sb("state", [128, 8 * 2 * 128], F32)
        state4 = state[:].rearrange("p (h s e) -> p h s e", s=2, e=128)
        osq = sb("osq", [128, T], BF16)
        rs = qTf[:, 2 * T:3 * T]
        sga = sb("sga", [128, 4 * T], BF16)
        sga3 = sga[:].rearrange("p (c t) -> p c t", t=T)
        sgh = sb("sgh", [128, 4 * T], BF16)
        sgh3 = sgh[:].rearrange("p (c t) -> p c t", t=T)
        m2 = sl
        mT = sb("mT", [128, 8 * T], BF16)
        mT3 = mT[:].rearrange("p (c t) -> p c t", t=T)
        ps = [es.enter_context(nc.psum_tensor("ps%d" % b, [128, 512], F32)) for b in range(8)]
        psb = [p_[:].bitcast(BF16) for p_ in ps]

        eng_sems = {e: sem("s_" + e) for e in ("pe", "act", "dve", "pool")}
        wsem = [sem("w%d" % s) for s in range(NS)]
        s32sem = [sem("s32_%d" % k) for k in range(2)]
        s16sem = [sem("s16_%d" % k) for k in range(2)]
        xsem = [sem("xl%d" % j) for j in range(TB)]
        osem = [sem("os%d" % j) for j in range(TB)]
        gsem = sem("gsem")
        csem = sem("csem")
        block = es.enter_context(nc.Block())

        def ps_r(b):
            return ("ps", b)

        def cdma(out_ap, in_ap, w, slow=False):
            if slow:
                P.op("pool", lambda e: e.dma_start(out=out_ap, in_=in_ap, allow_slow_non_contiguous=True),
                     writes=["cdma", w], dma_sem=csem)
            else:
                P.op("pool", lambda e: e.dma_start(out=out_ap, in_=in_ap), writes=["cdma", w], dma_sem=csem)

        b_in = W["b_in"]
        cdma(bfm[:, 0:16], b_in[0, C_FH:C_IH].rearrange("(c p) -> p c", p=128), "bfm", slow=True)
        cdma(bfm[:, 16:40], b_in[0, C_OG:7680].rearrange("(c p) -> p c", p=128), "bfm", slow=True)
        for li_, (gn_, bn_) in enumerate((("ln1_g", "ln1_b"), ("ln2_g", "ln2_b"))):
            cdma(gcol[:, li_ * 8:(li_ + 1) * 8], W[gn_][0, :].rearrange("(c p) -> p c", p=128), "gcol", slow=True)
            cdma(bcol[:, li_ * 8:(li_ + 1) * 8], W[bn_][0, :].rearrange("(c p) -> p c", p=128), "bcol", slow=True)
        cdma(lbl[:].rearrange("p (r h) -> p r h", h=8), W["hgrn_lb_logits"].rearrange("r (h p) -> p r h", p=128), "lbl", slow=True)
        cdma(ng[:], W["hgrn_norm_g"][0, :].rearrange("(p o) -> p o", o=1), "ng", slow=True)
        sk_t = W["attn_sinks"].tensor
        cdma(esk[0:64, :], bass.AP(sk_t, 0, [[0, 64], [2, 8]]), "esk", slow=True)
        cdma(esk[64:128, :], bass.AP(sk_t, 1, [[0, 64], [2, 8]]), "esk", slow=True)
        cdma(tmp8a[0:1, 0:1536], b_in[0:1, 0:1536], ("stg32", 0))
        cdma(tmp8b[0:1, 0:1024], b_in[0:1, C_IH:C_IH + 1024], ("stg32", 1))

        P.op("pool", lambda e: e.memset(ident[:], 1.0), writes=["ident"])
        P.op("pool", lambda e: e.affine_select(out=ident[:], in_=ident[:], pattern=[[-1, 128]], compare_op=ALU.is_equal,
                                                 fill=0.0, base=0, channel_multiplier=1), reads=["ident"], writes=["ident"])
        P.op("pool", lambda e: e.tensor_copy(out=identb[:], in_=ident[:]), reads=["ident"], writes=["identb"])
        P.op("pool", lambda e: e.memset(ones[:], 1.0), writes=["ones"])
        P.op("pool", lambda e: e.memset(onesm[:], 1.0 / 128.0), writes=["onesm"])
        P.op("pool", lambda e: e.memset(maskA[:], 1.0), writes=["maskA"])
        P.op("pool", lambda e: e.affine_select(out=maskA[:, 0:128], in_=maskA[:, 0:128], pattern=[[-1, 128]],
                                                 compare_op=ALU.is_ge, fill=0.0, base=-1, channel_multiplier=1),
             reads=["maskA"], writes=["maskA"])
        P.op("pool", lambda e: e.affine_select(out=maskA[:, 128:256], in_=maskA[:, 128:256], pattern=[[1, 128]],
                                                 compare_op=ALU.is_ge, fill=0.0, base=0, channel_multiplier=-1),
             reads=["maskA"], writes=["maskA"])
        P.op("pool", lambda e: e.memset(mask2[:], 1.0), writes=["mask2"])
        P.op("pool", lambda e: e.affine_select(out=mask2[:], in_=mask2[:], pattern=[[1, 128]], compare_op=ALU.is_ge,
                                                 fill=0.0, base=0, channel_multiplier=-1), reads=["mask2"], writes=["mask2"])
        P.op("pool", lambda e: e.memset(mask2[0:64, 64:128], 0.0), reads=["mask2"], writes=["mask2"])
        P.op("pool", lambda e: e.memset(rm[:], 1.0), writes=["rm"])
        P.op("pool", lambda e: e.memset(rm[:].rearrange("p (c t) -> p c t", t=64)[:, :, 0:1], 0.0), reads=["rm"], writes=["rm"])
        P.op("pool", lambda e: e.memset(epsc[:, 0:1], 4e-5), writes=["epsc"])
        P.op("pool", lambda e: e.memset(epsc[:, 1:2], 1e-5), reads=["epsc"], writes=["epsc"])
        P.op("pool", lambda e: e.memset(epsc[:, 2:3], 1e-6), reads=["epsc"], writes=["epsc"])
        P.op("pool", lambda e: e.memset(state[:], 0.0), writes=[("state", h_, s_) for h_ in range(8) for s_ in range(2)])
        P.op("dve", lambda e: e.tensor_tensor(out=lb[:], in0=lbl[:, 0:8], in1=lbl[:, 8:16], op=ALU.subtract),
             reads=["lbl"], writes=["lb"])
        P.op("act", lambda e: e.activation(out=lb[:], in_=lb[:], func=AF.Sigmoid), reads=["lb"], writes=["lb"])
        P.op("dve", lambda e: e.tensor_scalar(out=oml[:], in0=lb[:], scalar1=-1.0, scalar2=1.0, op0=ALU.mult, op1=ALU.add),
             reads=["lb"], writes=["oml"])
        P.op("dve", lambda e: e.tensor_scalar(out=noml[:], in0=lb[:], scalar1=-1.0, scalar2=None, op0=ALU.add),
             reads=["lb"], writes=["noml"])
        P.op("act", lambda e: e.activation(out=esk[:], in_=esk[:], func=AF.Exp), reads=["esk"], writes=["esk"])
        P.op("dve", lambda e: e.tensor_copy(out=brow[0:1, 0:1536], in_=tmp8a[0:1, 0:1536]), reads=[("stg32", 0)], writes=["brow"])
        P.op("dve", lambda e: e.tensor_copy(out=brow[0:1, 1536:2560], in_=tmp8b[0:1, 0:1024]), reads=[("stg32", 1), "brow"], writes=["brow"])
        CONST = ["ident", "identb", "ones", "onesm", "maskA", "mask2", "rm", "epsc", "bfm", "lb", "oml", "noml",
                 "ng", "esk", "brow", "cs"]
        for en in ("pe", "act", "dve", "pool"):
            pass

        cast_engs = ("act", "dve", "pool")
        pcount = [0]

        def convert_unit(kind, u):
            off = 0
            for pc in U[kind][u]:
                srcm, ka, kb, cols = pc
                nk = kb - ka
                Wd = sum(w for _, w in cols)
                n = nk * Wd
                k = pcount[0]
                pcount[0] += 1
                sbi = k % 2
                s32v = stg32[sbi][:, 0:n].rearrange("p (k w) -> p k w", w=Wd)
                srcv = srcm.rearrange("(kc p) f -> p kc f", p=128)
                o = 0
                for (c0, w) in cols:
                    P.op("sp", lambda e, o=o, w=w, c0=c0, s32v=s32v, srcv=srcv, ka=ka, kb=kb:
                         e.dma_start(out=s32v[:, :, o:o + w], in_=srcv[:, ka:kb, c0:c0 + w]),
                         writes=[("stg32", sbi)], dma_sem=s32sem[sbi])
                    o += w
                ce = cast_engs[k % 3]
                if ce == "act":
                    P.op("act", lambda e, sbi=sbi, n=n: e.activation(out=stg16[sbi][:, 0:n], in_=stg32[sbi][:, 0:n], func=AF.Copy),
                         reads=[("stg32", sbi)], writes=[("stg16", sbi)])
                else:
                    P.op(ce, lambda e, sbi=sbi, n=n: e.tensor_copy(out=stg16[sbi][:, 0:n], in_=stg32[sbi][:, 0:n]),
                         reads=[("stg32", sbi)], writes=[("stg16", sbi)])
                P.op("act", lambda e, sbi=sbi, n=n, off=off, kind=kind, u=u:
                     e.dma_start(out=scr[kind][u, :, off:off + n], in_=stg16[sbi][:, 0:n]),
                     reads=[("stg16", sbi)], writes=[("scr", kind, u)], dma_sem=s16sem[sbi])
                off += n

        for kind in KIND_ORDER:
            for u in range(len(U[kind])):
                convert_unit(kind, u)

        if DEBUG_STAGE == 1:
            seq_all = TILE_SEQ[:17] * NT
        elif DEBUG_STAGE == 2:
            seq_all = TILE_SEQ[:40] * NT
        elif DEBUG_STAGE in (21, 22, 23):
            seq_all = TILE_SEQ[:22] * NT
        elif DEBUG_STAGE == 24:
            seq_all = TILE_SEQ[:30] * NT
        else:
            seq_all = TILE_SEQ * NT

        class WL:
            next_emit = 0
            cur = -1

        def wl_emit():
            n = WL.next_emit
            if n >= len(seq_all):
                return
            kind, u = seq_all[n]
            s = n % NS
            E = UE[kind][u]
            P.op("sp", lambda e, s=s, E=E, kind=kind, u=u: e.dma_start(out=ws[s][:, 0:E], in_=scr[kind][u, :, 0:E]),
                 reads=[("scr", kind, u)], writes=[("ws", s)], dma_sem=wsem[s])
            WL.next_emit += 1

        def wl_acquire(kind, u):
            WL.cur += 1
            assert seq_all[WL.cur] == (kind, u), (seq_all[WL.cur], kind, u)
            while WL.next_emit <= WL.cur:
                wl_emit()
            return WL.cur % NS

        def wl_release():
            while WL.next_emit < min(WL.cur + 1 + NS, len(seq_all)):
                wl_emit()

        def load_x(i, j, extra_writes=()):
            r0 = i * T + j * 128
            P.op("pool", lambda e: e.dma_start(out=xb[j], in_=x[r0:r0 + 128, :]),
                 writes=[("xb", j)] + XB_ALIAS[j] + list(extra_writes), dma_sem=xsem[j])

        LN_GB = [("ln1_g", "ln1_b"), ("ln2_g", "ln2_b"), ("ln3_g", "ln3_b")]

        def load_gb(which):
            gname, bname = LN_GB[which]
            gt_ = W[gname].tensor
            bt_ = W[bname].tensor
            P.op("pool", lambda e: e.dma_start(out=gb3[:, 0, :], in_=bass.AP(gt_, 0, [[0, 128], [1, D]])),
                 writes=["gb"], dma_sem=gsem)
            P.op("pool", lambda e: e.dma_start(out=gb3[:, 1, :], in_=bass.AP(bt_, 0, [[0, 128], [1, D]])),
                 writes=["gb"], dma_sem=gsem)

        def mm(out_ap, lhsT, rhs, start, stop, reads, writes):
            P.op("pe", lambda e: e.matmul(out_ap, lhsT=lhsT, rhs=rhs, start=start, stop=stop), reads=reads, writes=writes)

        evac_rr = [0]

        def transposes_x(aff=None, from_xb=False):
            for c in range(KC):
                b = 4 + (c % 4)
                for j in range(TB):
                    if from_xb:
                        src_ = xb[j][:, c * 128:(c + 1) * 128]
                        rds_ = [("xb", j), "ident"] + XB_ALIAS[j]
                    else:
                        src_ = xa3[:, j, c * 128:(c + 1) * 128]
                        rds_ = [("xa", j), "ident"]
                    P.op("pe", lambda e, b=b, j=j, src_=src_: e.transpose(out=ps[b][:, j * 128:(j + 1) * 128], in_=src_, identity=ident[:]),
                         reads=rds_, writes=[ps_r(b)])
                if aff is None:
                    if c % 2 == 0:
                        P.op("act", lambda e, b=b, c=c: e.activation(out=xT3[:, c, :], in_=ps[b][:, :], func=AF.Copy),
                             reads=[ps_r(b)], writes=[("xT", c)])
                    else:
                        P.op("dve", lambda e, b=b, c=c: e.tensor_copy(out=xT3[:, c, :], in_=ps[b][:, :]),
                             reads=[ps_r(b)], writes=[("xT", c)])
                else:
                    gc_ = gcol[:, aff * 8 + c:aff * 8 + c + 1]
                    bc_ = bcol[:, aff * 8 + c:aff * 8 + c + 1]
                    if c % 2 == 0:
                        P.op("act", lambda e, b=b, c=c, gc_=gc_, bc_=bc_: e.activation(out=xT3[:, c, :], in_=ps[b][:, :], func=AF.Identity,
                                                                                   scale=gc_, bias=bc_),
                             reads=[ps_r(b), "gcol", "bcol"], writes=[("xT", c)])
                    else:
                        P.op("dve", lambda e, b=b, c=c, gc_=gc_, bc_=bc_: e.tensor_scalar(out=xT3[:, c, :], in0=ps[b][:, :], scalar1=gc_, scalar2=bc_,
                                                                                      op0=ALU.mult, op1=ALU.add),
                             reads=[ps_r(b), "gcol", "bcol"], writes=[("xT", c)])
            if aff is not None:
                for j in range(TB):
                    ln_gamma(j)
                for j in range(TB):
                    ln_beta(j)
                load_gb(aff + 1)

        def ln_stats(j, epsk):
            k = j
            P.op("dve", lambda e: e.bn_stats(out=st[k][:, 0:6], in_=xa3[:, j, 0:512]), reads=[("xa", j)], writes=[("st", k)])
            P.op("dve", lambda e: e.bn_stats(out=st[k][:, 6:12], in_=xa3[:, j, 512:1024]), reads=[("xa", j), ("st", k)],
                 writes=[("st", k)])
            P.op("dve", lambda e: e.bn_aggr(out=mv[k][:, 0:2], in_=st[k][:, 0:12]), reads=[("st", k)], writes=[("mv", k)])
            P.op("act", lambda e: e.activation(out=sd[k][:, 0:1], in_=mv[k][:, 1:2], func=AF.Sqrt, bias=epsc[:, epsk:epsk + 1], scale=1.0),
                 reads=[("mv", k), "epsc"], writes=[("sd", k)])
            P.op("dve", lambda e: e.reciprocal(out=rstd[k][:, 0:1], in_=sd[k][:, 0:1]), reads=[("sd", k)], writes=[("rstd", k)])
            P.op("dve", lambda e: e.scalar_tensor_tensor(out=nmr[k][:, 0:1], in0=mv[k][:, 0:1], scalar=-1.0, in1=rstd[k][:, 0:1],
                                                         op0=ALU.mult, op1=ALU.mult),
                 reads=[("mv", k), ("rstd", k)], writes=[("nmr", k)])

        def ln_norm(j):
            k = j
            P.op("act", lambda e: e.activation(out=xa3[:, j, :], in_=xa3[:, j, :], func=AF.Identity,
                                               scale=rstd[k][:, 0:1], bias=nmr[k][:, 0:1]),
                 reads=[("xa", j), ("rstd", k), ("nmr", k)], writes=[("xa", j)])

        def ln_gamma(j):
            P.op("pool", lambda e: e.tensor_tensor(out=xa3[:, j, :], in0=xa3[:, j, :], in1=gb3[:, 0, :], op=ALU.mult),
                 reads=[("xa", j), "gb"], writes=[("xa", j)])

        def ln_beta(j):
            P.op("dve", lambda e: e.tensor_tensor(out=xa3[:, j, :], in0=xa3[:, j, :], in1=gb3[:, 1, :], op=ALU.add),
                 reads=[("xa", j), "gb"], writes=[("xa", j)])

        def layernorm_all(epsk, after=None, defer_affine=False):
            for j in range(TB):
                ln_stats(j, epsk)
            for j in range(TB):
                ln_norm(j)
            if defer_affine:
                return
            for j in range(TB):
                ln_gamma(j)
            for j in range(TB):
                ln_beta(j)
                if after is not None:
                    after(j)

        def ffn(i, which):
            ka_, kb_ = ("f1a", "f1b") if which == 1 else ("f2a", "f2b")
            if which == 1:
                transposes_x(None, from_xb=True)
            else:
                transposes_x(1)
                if i + 1 < NT:
                    for j_ in range(TB):
                        load_x(i + 1, j_)
            for g in range(11):
                s = wl_acquire(ka_, g)
                wv = ws[s][:, :].rearrange("p (m k f) -> p m k f", m=2, k=8)
                for q in range(2):
                    fc = 2 * g + q
                    b1 = 2 * (fc % 2)
                    b3 = b1 + 1
                    for m, b in ((0, b1), (1, b3)):
                        for kc in range(KC):
                            mm(ps[b][:, :], wv[:, m, kc, q * 128:(q + 1) * 128], xT3[:, kc, :], kc == 0, kc == KC - 1,
                               [("ws", s), ("xT", kc)], [ps_r(b)])
                    kk = fc % 2
                    P.op("act", lambda e, b1=b1, kk=kk: e.activation(out=sl[kk][:, :], in_=ps[b1][:, :], func=AF.Silu),
                         reads=[ps_r(b1)], writes=[("sl", kk)])
                    P.op("dve", lambda e, b3=b3, kk=kk, fc=fc: e.tensor_tensor(out=gT3[:, fc, :], in0=sl[kk][:, :], in1=ps[b3][:, :], op=ALU.mult),
                         reads=[("sl", kk), ps_r(b3)], writes=[("gT", fc)])
                wl_release()
            for hf in range(2):
                banks = [4, 5, 6, 7] if hf == 0 else [0, 1, 2, 3]
                for ui, (k0, k1) in enumerate(((0, 8), (8, 16), (16, 22))):
                    s = wl_acquire(kb_, hf * 3 + ui)
                    wv = ws[s][:, :].rearrange("p (k f) -> p k f", f=512)
                    for kc in range(k0, k1):
                        for j in range(TB):
                            mm(ps[banks[j]][:, :], gT3[:, kc, j * 128:(j + 1) * 128], wv[:, kc - k0, :], kc == 0, kc == FC - 1,
                               [("ws", s), ("gT", kc)], [ps_r(banks[j])])
                    wl_release()
                for j in range(TB):
                    bj = banks[j]
                    if which == 1:
                        rsrc = xb[j][:, hf * 512:(hf + 1) * 512]
                        rrd = [("xb", j), ("xa", j), ps_r(bj)] + XB_ALIAS[j]
                    else:
                        rsrc = xa3[:, j, hf * 512:(hf + 1) * 512]
                        rrd = [("xa", j), ps_r(bj)]
                    P.op("dve", lambda e, j=j, bj=bj, hf=hf, rsrc=rsrc: e.scalar_tensor_tensor(
                        out=xa3[:, j, hf * 512:(hf + 1) * 512], in0=rsrc, scalar=2.0 * ALPHA,
                        in1=ps[bj][:, :], op0=ALU.mult, op1=ALU.add),
                        reads=rrd, writes=[("xa", j)])
            def after_ln(j):
                if which == 2:
                    r0 = i * T + j * 128
                    P.op("pool", lambda e, j=j, r0=r0: e.dma_start(out=out[r0:r0 + 128, :], in_=xa3[:, j, :]),
                         reads=[("xa", j)], dma_sem=osem[j])
            if which == 1:
                layernorm_all(0, None, defer_affine=True)
            else:
                layernorm_all(0, after_ln)
                load_gb(0)

        def rope_ops(b, ncols_heads, gbk, out_views):
            nh = ncols_heads
            psv = ps[b][:, 0:nh * 64].rearrange("p (h d) -> p h d", d=64)
            rqv = rq[:, 0:nh * 16].rearrange("p (h d) -> p h d", d=16)
            P.op("act", lambda e: e.activation(out=rqv, in_=psv[:, :, 0:16], func=AF.Copy),
                 reads=[ps_r(b)], writes=["rq"])
            t1 = rqv[:, :, 0:8]
            t2 = rqv[:, :, 8:16]
            jj_ = gbk % TB
            cosb = cst3[:, jj_, 0:64].rearrange("p (h d) -> p h d", d=8)[:, 0:nh, :]
            sinb = cst3[:, jj_, 64:128].rearrange("p (h d) -> p h d", d=8)[:, 0:nh, :]
            for a, (tt, cc) in enumerate(((t1, cosb), (t2, sinb), (t2, cosb), (t1, sinb))):
                P.op("dve", lambda e, a=a, tt=tt, cc=cc: e.tensor_tensor(out=rt4[:, a, 0:nh, :], in0=tt, in1=cc, op=ALU.mult),
                     reads=["rq", "cst", ("rt", a)], writes=[("rt", a)])
            return nh

        def mixer(i):
            xsem_ = csem
            for j_ in range(TB):
                r0_ = i * T + j_ * 128
                P.op("pool", lambda e, j_=j_, r0_=r0_: e.dma_start(out=cst3[:, j_, :], in_=rope[r0_:r0_ + 128, :]),
                     writes=["cst", "cdma"], dma_sem=csem)
            transposes_x(0)
            for u in range(5):
                s = wl_acquire("tm", u)
                wv = ws[s][:, :].rearrange("p (k f) -> p k f", f=512)
                for j in range(TB):
                    b = (u * TB + j) % 2
                    gbk = i * TB + j
                    for kc in range(KC):
                        mm(ps[b][:, :], xT3[:, kc, j * 128:(j + 1) * 128], wv[:, kc, :], kc == 0, ("NOK1" in os.environ and kc == KC - 1),
                           [("ws", s), ("xT", kc)], [ps_r(b)])
                    if "NOK1" not in os.environ:
                        mm(ps[b][:, :], ones[0:1, 0:128], brow[0:1, u * 512:(u + 1) * 512], False, True,
                           ["ones", "brow"], [ps_r(b)])
                    if u < 2:
                        P.op("act", lambda e, b=b, j=j, u=u: e.activation(out=qkb3[:, j, u * 512:(u + 1) * 512], in_=ps[b][:, :], func=AF.Copy),
                             reads=[ps_r(b)], writes=[("qkb", j)])
                        rope_ops(b, 8, gbk, None)
                        qv = qkb3[:, j, 0:1024].rearrange("p (h d) -> p h d", d=64)
                        P.op("dve", lambda e, qv=qv, u=u: e.tensor_tensor(out=qv[:, 8 * u:8 * u + 8, 0:8], in0=rt4[:, 0, :, :], in1=rt4[:, 1, :, :], op=ALU.subtract),
                             reads=[("rt", 0), ("rt", 1)], writes=[("qkb", j)])
                        P.op("dve", lambda e, qv=qv, u=u: e.tensor_tensor(out=qv[:, 8 * u:8 * u + 8, 8:16], in0=rt4[:, 2, :, :], in1=rt4[:, 3, :, :], op=ALU.add),
                             reads=[("rt", 2), ("rt", 3)], writes=[("qkb", j)])
                    elif u == 2:
                        kd = qkb3[:, j, 1024:1536].rearrange("p (h r d) -> p h r d", r=2, d=64)
                        kin3 = ps[b][:, 0:256].rearrange("p (h d) -> p h d", d=64)
                        P.op("act", lambda e, kd=kd, kin3=kin3: e.activation(out=kd[:, :, 0, :], in_=kin3, func=AF.Copy),
                             reads=[ps_r(b)], writes=[("qkb", j)])
                        P.op("dve", lambda e, kd=kd, kin3=kin3: e.tensor_copy(out=kd[:, :, 1, :], in_=kin3),
                             reads=[ps_r(b)], writes=[("qkb", j)])
                        P.op("dve", lambda e, b=b, j=j: e.tensor_copy(out=vt3[:, j + 1, :], in_=ps[b][:, 256:512]),
                             reads=[ps_r(b)], writes=[("vt", j + 1)])
                        rope_ops(b, 4, gbk, None)
                        for r_ in range(2):
                            P.op("dve", lambda e, kd=kd, r_=r_: e.tensor_tensor(
                                out=kd[:, :, r_, 0:8], in0=rt4[:, 0, 0:4, :], in1=rt4[:, 1, 0:4, :], op=ALU.subtract),
                                reads=[("rt", 0), ("rt", 1)], writes=[("qkb", j)])
                            P.op("dve", lambda e, kd=kd, r_=r_: e.tensor_tensor(
                                out=kd[:, :, r_, 8:16], in0=rt4[:, 2, 0:4, :], in1=rt4[:, 3, 0:4, :], op=ALU.add),
                                reads=[("rt", 2), ("rt", 3)], writes=[("qkb", j)])
                    else:
                        hcol = (u - 3) * 512
                        if j % 2 == 0:
                            P.op("act", lambda e, b=b, j=j, hcol=hcol: e.activation(out=iht3[:, j, hcol:hcol + 512], in_=ps[b][:, :], func=AF.Copy),
                                 reads=[ps_r(b)], writes=[("iht", j)])
                        else:
                            P.op("dve", lambda e, b=b, j=j, hcol=hcol: e.tensor_copy(out=iht3[:, j, hcol:hcol + 512], in_=ps[b][:, :]),
                                 reads=[ps_r(b)], writes=[("iht", j)])
                wl_release()
            if DEBUG_STAGE == 21:
                return
            for j in range(TB):
                for c in range(8):
                    P.op("pe", lambda e, j=j, c=c: e.transpose(out=psb[2][:, c * 128:(c + 1) * 128], in_=qkb3[:, j, c * 128:(c + 1) * 128], identity=identb[:]),
                         reads=[("qkb", j), "identb"], writes=[ps_r(2)])
                P.op("act", lambda e, j=j: e.activation(out=qT3[:, :, j * 128:(j + 1) * 128],
                                                        in_=psb[2][:, 0:1024].rearrange("p (c t) -> p c t", t=128), func=AF.Copy),
                     reads=[ps_r(2)], writes=[("qT", j)])
                for g in range(4):
                    P.op("pe", lambda e, j=j, g=g: e.transpose(out=psb[3][:, g * 128:(g + 1) * 128],
                                                               in_=qkb3[:, j, 1024 + g * 128:1024 + (g + 1) * 128], identity=identb[:]),
                         reads=[("qkb", j), "identb"], writes=[ps_r(3)])
                P.op("dve", lambda e, j=j: e.tensor_copy(out=kT3[:, :, (j + 1) * 128:(j + 2) * 128],
                                                         in_=psb[3][:, 0:512].rearrange("p (g t) -> p g t", t=128)),
                     reads=[ps_r(3)], writes=[("kT", j + 1)])
            if DEBUG_STAGE == 22:
                return
            pairs = [(j_, c_) for j_ in range(TB) for c_ in range(8)]

            def att_stage1(k):
                j, c = pairs[k]
                first = (i == 0 and j == 0)
                kbs = [1] if first else [0, 1]
                par = k % 2
                g = c // 2
                sbanks = (par, 4 + par)
                for hd in range(2):
                    pb = 64 * hd
                    sbk = sbanks[hd]
                    for kb in kbs:
                        mm(ps[sbk][:, kb * 128:(kb + 1) * 128],
                           kT3[pb:pb + 64, g, (j + kb) * 128:(j + kb + 1) * 128], qT3[pb:pb + 64, c, j * 128:(j + 1) * 128],
                           True, True, [("kT", j + kb), ("qT", j)], [ps_r(sbk)])
                    c0_ = 128 if first else 0
                    P.op("act", lambda e, sbk=sbk, hd=hd, c0_=c0_, par=par: e.activation(
                        out=pexp[par][:, hd * 256 + c0_:(hd + 1) * 256], in_=ps[sbk][:, c0_:256], func=AF.Exp, scale=0.125),
                        reads=[ps_r(sbk)], writes=[("pexp", par)])
                if first:
                    pev = pexp[par][:, :].rearrange("p (h n) -> p h n", n=256)[:, :, 128:256]
                    ptv = PT[par][:, :].rearrange("p (h n) -> p h n", n=256)[:, :, 128:256]
                    mk = maskA[:, 128:256].unsqueeze(1).broadcast_to([128, 2, 128])
                else:
                    pev = pexp[par][:, :].rearrange("p (h n) -> p h n", n=256)
                    ptv = PT[par][:, :].rearrange("p (h n) -> p h n", n=256)
                    mk = maskA[:, 0:256].unsqueeze(1).broadcast_to([128, 2, 256])
                P.op("dve", lambda e, ptv=ptv, pev=pev, mk=mk: e.tensor_tensor(out=ptv, in0=pev, in1=mk, op=ALU.mult),
                     reads=[("pexp", par), "maskA"], writes=[("PT", par)])

            def att_stage2(k):
                j, c = pairs[k]
                first = (i == 0 and j == 0)
                kbs = [1] if first else [0, 1]
                par = k % 2
                ob = 6 + par
                g = c // 2
                for which_ in range(2):
                    for hd in range(2):
                        pb = 64 * hd
                        for idx, kb in enumerate(kbs):
                            lhs = vt3[:, j + kb, g * 64:(g + 1) * 64] if which_ == 0 else ones[:, 0:64]
                            rd = [("vt", j + kb), ("PT", par)] if which_ == 0 else ["ones", ("PT", par)]
                            mm(ps[ob][pb:pb + 64, which_ * 128:(which_ + 1) * 128], lhs,
                               PT[par][:, hd * 256 + kb * 128: hd * 256 + (kb + 1) * 128],
                               idx == 0, idx == len(kbs) - 1, rd, [ps_r(ob)])
                P.op("act", lambda e: e.activation(out=rec[par][:, :], in_=ps[ob][:, 128:256], func=AF.Ln, bias=esk[:, c:c + 1], scale=1.0),
                     reads=[ps_r(ob), "esk"], writes=[("rec", par)])
                P.op("act", lambda e: e.activation(out=rec[par][:, :], in_=rec[par][:, :], func=AF.Exp, scale=-1.0),
                     reads=[("rec", par)], writes=[("rec", par)])
                P.op("dve", lambda e: e.tensor_tensor(out=yaT3[:, c, j * 128:(j + 1) * 128], in0=ps[ob][:, 0:128],
                                                      in1=rec[par][:, :], op=ALU.mult),
                     reads=[ps_r(ob), ("rec", par)], writes=[("yaT", c, j)])

            att_stage1(0)
            for k_ in range(len(pairs)):
                if k_ + 1 < len(pairs):
                    att_stage1(k_ + 1)
                att_stage2(k_)
            P.op("pool", lambda e: e.tensor_copy(out=kT3[:, :, 0:128], in_=kT3[:, :, 512:640]), reads=[("kT", 4)], writes=[("kT", 0)])
            P.op("pool", lambda e: e.tensor_copy(out=vt3[:, 0, :], in_=vt3[:, 4, :]), reads=[("vt", 4)], writes=[("vt", 0)])

            if DEBUG_STAGE == 23:
                return
            def Pstage(h, mid_hook=None):
                s = wl_acquire("hg", h)
                wv = ws[s][:, 0:8 * 384].rearrange("p (k f) -> p k f", f=384)
                for m in range(3):
                    for kc in range(KC):
                        mm(ps[m][:, :], wv[:, kc, m * 128:(m + 1) * 128], xT3[:, kc, :], kc == 0, kc == KC - 1,
                           [("ws", s), ("xT", kc)], [ps_r(m)])
                    if m == 1 and mid_hook is not None:
                        mid_hook()
                wl_release()

            def Estage(h):
                par = h % 3
                P.op("act", lambda e: e.activation(out=hA, in_=ps[0][:, :], func=AF.Sigmoid, bias=bfm[:, h:h + 1], scale=1.0),
                     reads=[ps_r(0), "bfm"], writes=["hA"])
                P.op("act", lambda e: e.activation(out=hD, in_=ps[1][:, :], func=AF.Silu, bias=bfm[:, 8 + h:9 + h], scale=1.0),
                     reads=[ps_r(1), "bfm"], writes=["hD"])
                P.op("act", lambda e: e.activation(out=sog[par][:, :], in_=ps[2][:, :], func=AF.Silu, bias=bfm[:, 16 + h:17 + h], scale=1.0),
                     reads=[ps_r(2), "bfm"], writes=[("sog", par)])
                P.op("act", lambda e: e.activation(out=hB, in_=hA, func=AF.Ln, bias=lb[:, h:h + 1], scale=oml[:, h:h + 1]),
                     reads=["hA", "lb", "oml"], writes=["hB"])
                P.op("dve", lambda e: e.tensor_scalar(out=hA, in0=hA, scalar1=noml[:, h:h + 1], scalar2=oml[:, h:h + 1], op0=ALU.mult, op1=ALU.add),
                     reads=["hA", "noml", "oml"], writes=["hA"])
                P.op("dve", lambda e: e.tensor_tensor_scan(out=hC, data0=rm[:, :], data1=hB, initial=0.0, op0=ALU.mult, op1=ALU.add),
                     reads=["rm", "hB"], writes=["hC"])
                P.op("act", lambda e: e.activation(out=dec[par][:, :], in_=hC.rearrange("p (c t) -> p c t", t=64)[:, :, 63], func=AF.Exp),
                     reads=["hC"], writes=[("dec", par)])
                P.op("act", lambda e: e.activation(out=hB, in_=hC, func=AF.Exp), reads=["hC"], writes=["hB"])
                P.op("act", lambda e: e.activation(out=hC, in_=hC, func=AF.Exp, scale=-1.0), reads=["hC"], writes=["hC"])
                P.op("dve", lambda e: e.tensor_tensor(out=qdec[par][:, :], in0=hD, in1=hB, op=ALU.mult),
                     reads=["hD", "hB"], writes=[("qdec", par)])
                P.op("dve", lambda e: e.tensor_tensor(out=hA, in0=hA, in1=hC, op=ALU.mult), reads=["hA", "hC"], writes=["hA"])
                P.op("dve", lambda e: e.tensor_copy(out=kinv[par][:, :], in_=hA), reads=["hA"], writes=[("kinv", par)])
                P.op("dve", lambda e: e.tensor_tensor(out=kend[par][:, :].rearrange("p (c t) -> p c t", t=64),
                                                      in0=hA.rearrange("p (c t) -> p c t", t=64),
                                                      in1=dec[par][:, :].unsqueeze(2).broadcast_to([128, 8, 64]), op=ALU.mult),
                     reads=["hA", ("dec", par)], writes=[("kend", par)])

            def Xa_stage(h):
                par = h % 2
                p3 = h % 3
                for j in range(TB):
                    P.op("pe", lambda e, j=j: e.transpose(out=psb[3][:, j * 128:(j + 1) * 128], in_=kend[p3][:, j * 128:(j + 1) * 128], identity=identb[:]),
                         reads=[("kend", p3), "identb"], writes=[ps_r(3)])
                P.op("act", lambda e: e.activation(out=kendTM[par][:, :], in_=psb[3][:, 0:512], func=AF.Copy),
                     reads=[ps_r(3)], writes=[("kendTM", par)])
                for j in range(TB):
                    mm(ps[4][:, j * 128:(j + 1) * 128], kinv[p3][:, j * 128:(j + 1) * 128], qdec[p3][:, j * 128:(j + 1) * 128], True, True,
                       [("kinv", p3), ("qdec", p3)], [ps_r(4)])
                P.op("dve", lambda e: e.tensor_tensor(out=msk[par][:, :].rearrange("p (j t) -> p j t", t=128),
                                                      in0=ps[4][:, :].rearrange("p (j t) -> p j t", t=128),
                                                      in1=mask2[:, :].unsqueeze(1).broadcast_to([128, 4, 128]), op=ALU.mult),
                     reads=[ps_r(4), "mask2"], writes=[("msk", par)])

            def Xb_stage(h):
                par = h % 2
                p3 = h % 3
                P.op("dve", lambda e: e.tensor_copy(out=S16[par][:, 0:128], in_=state4[:, h, 0, :]),
                     reads=[("state", h, 0)], writes=[("S16", par, 0)])
                ktm = kendTM[par][:, :].rearrange("p (j d) -> p j d", d=128)
                for half in range(2):
                    for cc in range(4 * half, 4 * half + 4):
                        jj = cc // 2
                        p0 = 64 * (cc % 2)
                        ub = 5 if cc % 2 == 0 else 4
                        uc = (cc // 2) * 128
                        mm(ps[ub][:, uc:uc + 128], ktm[p0:p0 + 64, jj, :], iht3[p0:p0 + 64, jj, h * 128:(h + 1) * 128], True, True,
                           [("kendTM", par), ("iht", jj)], [ps_r(ub)])
                    for cc in range(4 * half, 4 * half + 4):
                        si = cc % 2
                        so = (cc + 1) % 2
                        ub = 5 if cc % 2 == 0 else 4
                        uc = (cc // 2) * 128
                        P.op("dve", lambda e, cc=cc, si=si, so=so, ub=ub, uc=uc: e.scalar_tensor_tensor(
                            out=state4[:, h, so, :], in0=state4[:, h, si, :], scalar=dec[p3][:, cc:cc + 1],
                            in1=ps[ub][:, uc:uc + 128], op0=ALU.mult, op1=ALU.add),
                            reads=[("state", h, si), ("dec", p3), ps_r(ub)], writes=[("state", h, so)])
                        if cc < 7:
                            P.op("dve", lambda e, cc=cc, so=so: e.tensor_copy(out=S16[par][:, (cc + 1) * 128:(cc + 2) * 128], in_=state4[:, h, so, :]),
                                 reads=[("state", h, so)], writes=[("S16", par, cc + 1)])

            def Oa_stage(h):
                par = h % 2
                p3 = h % 3
                for j in range(TB):
                    mm(ps[6][:, j * 128:(j + 1) * 128], iht3[:, j, h * 128:(h + 1) * 128], msk[par][:, j * 128:(j + 1) * 128], True, False,
                       [("iht", j), ("msk", par)], [ps_r(6)])
                    for q in range(2):
                        cc = 2 * j + q
                        mm(ps[6][:, j * 128 + q * 64: j * 128 + (q + 1) * 64], S16[par][:, cc * 128:(cc + 1) * 128],
                           qdec[p3][:, j * 128 + q * 64: j * 128 + (q + 1) * 64], False, q == 1,
                           [("S16", par, cc), ("qdec", p3)], [ps_r(6)])
                P.op("act", lambda e: e.activation(out=osq[:, :], in_=ps[6][:, :], func=AF.Square), reads=[ps_r(6)], writes=["osq"])

            def Ob_stage(h):
                par = h % 2
                p3 = h % 3
                mm(ps[7][:, :], onesm[:, :], osq[:, :], True, True, ["onesm", "osq"], [ps_r(7)])
                P.op("act", lambda e: e.activation(out=rs[:, :], in_=ps[7][:, :], func=AF.Ln, bias=epsc[:, 2:3], scale=1.0),
                     reads=[ps_r(7), "epsc"], writes=["rs"])
                P.op("act", lambda e: e.activation(out=rs[:, :], in_=rs[:, :], func=AF.Exp, scale=-0.5), reads=["rs"], writes=["rs"])
                P.op("dve", lambda e: e.scalar_tensor_tensor(out=rs[:, :], in0=ps[6][:, :], scalar=ng[:, 0:1], in1=rs[:, :], op0=ALU.mult, op1=ALU.mult),
                     reads=[ps_r(6), "ng", "rs"], writes=["rs"])
                P.op("pool", lambda e: e.tensor_tensor(out=yhT3[:, h, :], in0=rs[:, :], in1=sog[p3][:, :], op=ALU.mult),
                     reads=["rs", ("sog", p3)], writes=[("yhT", h)])

            for step in range(11):
                ho = step - 3
                hx = step - 2
                if 0 <= ho < 8:
                    Oa_stage(ho)
                if 0 <= hx < 8:
                    Xa_stage(hx)

                def mid(ho=ho, hx=hx):
                    if 0 <= ho < 8:
                        Ob_stage(ho)
                    if 0 <= hx < 8:
                        Xb_stage(hx)
                if step < 8:
                    Pstage(step, mid)
                    Estage(step)
                else:
                    mid()

            if DEBUG_STAGE == 24:
                return
            bankset = [0]

            def next_banks():
                bs = [0, 1, 2, 3] if bankset[0] % 2 == 0 else [4, 5, 6, 7]
                bankset[0] += 1
                return bs

            for gq in range(2):
                for (kind, u, role) in (("g", gq, "ga"), ("p", gq, "pa"), ("g", 2 + gq, "gh"), ("p", 2 + gq, "ph")):
                    s = wl_acquire(kind, u)
                    wv = ws[s][:, :].rearrange("p (k f) -> p k f", f=512)
                    bs = next_banks()
                    for cc in range(4):
                        c = 4 * gq + cc
                        b = bs[cc]
                        for kc in range(KC):
                            if role in ("ga", "gh"):
                                rhs, rr = xT3[:, kc, :], ("xT", kc)
                                rds = [("ws", s), rr]
                            elif role == "pa":
                                rhs = yaT3[:, kc, :]
                                rds = [("ws", s)] + [("yaT", kc, jj) for jj in range(TB)]
                            else:
                                rhs = yhT3[:, kc, :]
                                rds = [("ws", s), ("yhT", kc)]
                            mm(ps[b][:, :], wv[:, kc, cc * 128:(cc + 1) * 128], rhs, kc == 0, kc == KC - 1, rds, [ps_r(b)])
                        if role == "ga":
                            P.op("act", lambda e, b=b, cc=cc, c=c: e.activation(out=sga3[:, cc, :], in_=ps[b][:, :], func=AF.Sigmoid, bias=bfm[:, 24 + c:25 + c], scale=1.0),
                                 reads=[ps_r(b), "bfm"], writes=[("sga", cc)])
                        elif role == "gh":
                            P.op("act", lambda e, b=b, cc=cc, c=c: e.activation(out=sgh3[:, cc, :], in_=ps[b][:, :], func=AF.Sigmoid, bias=bfm[:, 32 + c:33 + c], scale=1.0),
                                 reads=[ps_r(b), "bfm"], writes=[("sgh", cc)])
                        elif role == "pa":
                            P.op("dve", lambda e, b=b, cc=cc: e.tensor_tensor(out=m13[:, cc, :], in0=ps[b][:, :], in1=sga3[:, cc, :], op=ALU.mult),
                                 reads=[ps_r(b), ("sga", cc)], writes=[("m1", cc)])
                        else:
                            kk = cc % 2
                            P.op("dve", lambda e, b=b, cc=cc, kk=kk: e.tensor_tensor(out=m2[kk][:, :], in0=ps[b][:, :], in1=sgh3[:, cc, :], op=ALU.mult),
                                 reads=[ps_r(b), ("sgh", cc)], writes=[("sl", kk)])
                            P.op("pool", lambda e, cc=cc, kk=kk, c=c: e.tensor_tensor(out=mT3[:, c, :], in0=m13[:, cc, :], in1=m2[kk][:, :], op=ALU.add),
                                 reads=[("m1", cc), ("sl", kk)], writes=[("mT", c)])
                    wl_release()
            for hf in range(2):
                s = wl_acquire("o", hf)
                wv = ws[s][:, :].rearrange("p (k f) -> p k f", f=512)
                bs = next_banks()
                for j in range(TB):
                    for kc in range(KC):
                        mm(ps[bs[j]][:, :], mT3[:, kc, j * 128:(j + 1) * 128], wv[:, kc, :], kc == 0, kc == KC - 1,
                           [("ws", s), ("mT", kc)], [ps_r(bs[j])])
                wl_release()
                for j in range(TB):
                    bj = bs[j]
                    P.op("dve", lambda e, j=j, bj=bj, hf=hf: e.scalar_tensor_tensor(
                        out=xa3[:, j, hf * 512:(hf + 1) * 512], in0=xa3[:, j, hf * 512:(hf + 1) * 512], scalar=ALPHA,
                        in1=ps[bj][:, :], op0=ALU.mult, op1=ALU.add),
                        reads=[("xa", j), ps_r(bj)], writes=[("xa", j)])
            layernorm_all(1, None, defer_affine=True)

        for n_ in range(NS):
            wl_emit()
        fence = [("stg32", 0), ("stg32", 1), ("stg16", 0), ("stg16", 1)]
        for j in range(TB):
            load_x(0, j, extra_writes=fence)
        load_gb(0)
        def dbg_store(i):
            for j in range(TB):
                r0 = i * T + j * 128
                P.op("pool", lambda e, j=j, r0=r0: e.dma_start(out=out[r0:r0 + 128, :], in_=xa3[:, j, :]),
                     reads=[("xa", j)], dma_sem=osem[j])
                if i + 1 < NT:
                    load_x(i + 1, j)

        for i in range(NT):
            ffn(i, 1)
            if DEBUG_STAGE == 1:
                dbg_store(i)
                continue
            mixer(i)
            if DEBUG_STAGE >= 2:
                dbg_store(i)
                continue
            ffn(i, 2)
        for j in range(TB):
            P.op("pool", lambda e, j=j: e.wait_ge(osem[j], 16 * NT))

        P.finalize(eng_sems)

        @block.sync
        def _(e):
            P.emit("sp", e)

        @block.tensor
        def _(e):
            P.emit("pe", e)

        @block.scalar
        def _(e):
            P.emit("act", e)

        @block.vector
        def _(e):
            P.emit("dve", e)

        @block.gpsimd
        def _(e):
            P.emit("pool", e)
    return nc


def rope_table(S):
    pos = np.arange(S, dtype=np.float32)
    inv = (np.float32(500000.0) ** (-(np.arange(0, 16, 2, dtype=np.float32)) / np.float32(16))).astype(np.float32)
    ang = (pos[:, None] * inv[None, :]).astype(np.float32)
    c = np.tile(np.cos(ang).astype(np.float32), (1, 8))
    s_ = np.tile(np.sin(ang).astype(np.float32), (1, 8))
    return np.ascontiguousarray(np.concatenate([c, s_], axis=1).astype(np.float32))


_CACHE = {}


def run(inputs, S, n_cores):
    if S not in _CACHE:
        _CACHE[S] = build(S)
    nc = _CACHE[S]
    x = np.asarray(inputs["x"], dtype=np.float32)
    wmap = {}
    for n in WNAMES:
        a = np.asarray(inputs[n], dtype=np.float32)
        if n != "hgrn_lb_logits":
            a = a.reshape(WSHAPES[n])
        wmap[n] = np.ascontiguousarray(a)
    rt_ = rope_table(S)
    in_maps = []
    for c in range(n_cores):
        m = {"x": np.ascontiguousarray(x[c]), "rope_cs": rt_}
        m.update(wmap)
        in_maps.append(m)
    res = run_bass_kernel_spmd(nc, in_maps, core_ids=list(range(n_cores)))
    return np.stack([np.asarray(r["out"]) for r in res.results], axis=0)


def kernel(**inputs):
    x = inputs["x"]
    B, S, _ = x.shape
    return run(inputs, S, B).astype(np.float32)
```

```python
import contextlib
import os
import numpy as np
import concourse.bass as bass
import concourse.mybir as mybir
from concourse.bass_utils import run_bass_kernel_spmd

F32 = mybir.dt.float32
BF16 = mybir.dt.bfloat16
AF = mybir.ActivationFunctionType
ALU = mybir.AluOpType

ENGS = ("pe", "act", "dve", "pool", "sp")
SAME_ENGINE_SYNC = {"pe": False, "act": True, "dve": True, "pool": True, "sp": False}


class Op:
    __slots__ = ("eng", "fn", "deps", "signal", "tok", "dma_sem", "idx")

    def __init__(self, eng, fn, dma_sem=None):
        self.eng = eng
        self.fn = fn
        self.deps = []
        self.signal = False
        self.tok = None
        self.dma_sem = dma_sem


class Res:
    __slots__ = ("w", "readers")

    def __init__(self):
        self.w = None
        self.readers = {}


class Prog:
    def __init__(self):
        self.ops = {e: [] for e in ENGS}
        self.res = {}
        self.nops = 0

    def _need(self, d, o):
        if d.dma_sem is not None:
            return True
        if d.eng != o.eng:
            return True
        return SAME_ENGINE_SYNC[o.eng]

    def op(self, eng, fn, reads=(), writes=(), dma_sem=None):
        o = Op(eng, fn, dma_sem)
        o.idx = self.nops
        self.nops += 1
        deps = {}
        for r in reads:
            st = self.res.get(r)
            if st is not None and st.w is not None:
                deps[id(st.w)] = st.w
        for w in writes:
            st = self.res.get(w)
            if st is not None:
                if st.w is not None:
                    deps[id(st.w)] = st.w
                for rd in st.readers.values():
                    deps[id(rd)] = rd
        for d in deps.values():
            if d is not o and self._need(d, o):
                o.deps.append(d)
                d.signal = True
        for r in reads:
            st = self.res.get(r)
            if st is None:
                st = self.res[r] = Res()
            key = o.eng if dma_sem is None else ("dma", o.idx)
            st.readers[key] = o
        for w in writes:
            st = self.res.get(w)
            if st is None:
                st = self.res[w] = Res()
            st.w = o
            st.readers = {}
        self.ops[eng].append(o)
        return o

    def finalize(self, eng_sems):
        dma_counts = {}
        for e in ENGS:
            c = 0
            for o in self.ops[e]:
                if o.dma_sem is not None:
                    k = id(o.dma_sem)
                    dma_counts[k] = dma_counts.get(k, 0) + 16
                    o.tok = (o.dma_sem, dma_counts[k])
                elif o.signal:
                    c += 1
                    o.tok = (eng_sems[e], c)

    def emit(self, eng_name, eng):
        waited = {}
        for o in self.ops[eng_name]:
            need = {}
            for d in o.deps:
                sem, val = d.tok
                k = id(sem)
                if waited.get(k, 0) < val and need.get(k, (None, 0))[1] < val:
                    need[k] = (sem, val)
            for k, (sem, val) in need.items():
                eng.wait_ge(sem, val)
                waited[k] = val
            ins = o.fn(eng)
            if o.dma_sem is not None:
                ins.then_inc(o.dma_sem, 16)
            elif o.signal:
                ins.then_inc(o.tok[0], 1)


D = 1024
KC = 8
DFF = 2816
FC = 22
T = 512
TB = 4
NS = 4
DEBUG_STAGE = 0
ALPHA = 2.0 ** 0.25
C_QA, C_KA, C_VA, C_FH, C_QH, C_IH, C_OG, C_GA, C_GH = 0, 1024, 1280, 1536, 2560, 3584, 4608, 5632, 6656
WNAMES = ["ln1_g", "ln1_b", "ffn1_w1", "ffn1_w3", "ffn1_w2", "ln2_g", "ln2_b", "w_in", "b_in",
          "attn_sinks", "hgrn_lb_logits", "hgrn_norm_g", "w_proj_attn", "w_proj_hgrn", "w_out",
          "ln3_g", "ln3_b", "ffn2_w1", "ffn2_w3", "ffn2_w2"]
WSHAPES = {"ln1_g": [1, D], "ln1_b": [1, D], "ffn1_w1": [D, DFF], "ffn1_w3": [D, DFF], "ffn1_w2": [DFF, D],
           "ln2_g": [1, D], "ln2_b": [1, D], "w_in": [D, 7680], "b_in": [1, 7680], "attn_sinks": [1, 16],
           "hgrn_lb_logits": [2, 1024], "hgrn_norm_g": [1, 128], "w_proj_attn": [D, D], "w_proj_hgrn": [D, D],
           "w_out": [D, D], "ln3_g": [1, D], "ln3_b": [1, D], "ffn2_w1": [D, DFF], "ffn2_w3": [D, DFF],
           "ffn2_w2": [DFF, D]}


def mk_units(W):
    U = {}

    def w13(w1, w3):
        return [[(w1, 0, 8, [(g * 256, 256)]), (w3, 0, 8, [(g * 256, 256)])] for g in range(11)]

    def w2u(w2):
        us = []
        for hf in range(2):
            for (k0, k1) in ((0, 8), (8, 16), (16, 22)):
                pcs = []
                k = k0
                while k < k1:
                    ke = min(k + 4, k1)
                    pcs.append((w2, k, ke, [(hf * 512, 512)]))
                    k = ke
                us.append(pcs)
        return us

    def colu(w, c0):
        return [(w, 0, 4, [(c0, 512)]), (w, 4, 8, [(c0, 512)])]

    U["f1a"] = w13(W["ffn1_w1"], W["ffn1_w3"])
    U["f1b"] = w2u(W["ffn1_w2"])
    U["f2a"] = w13(W["ffn2_w1"], W["ffn2_w3"])
    U["f2b"] = w2u(W["ffn2_w2"])
    win = W["w_in"]
    U["tm"] = [colu(win, C_QA), colu(win, C_QA + 512), colu(win, C_KA), colu(win, C_IH), colu(win, C_IH + 512)]
    U["hg"] = [[(win, ka, ka + 4, [(C_FH + 128 * h, 128), (C_QH + 128 * h, 128), (C_OG + 128 * h, 128)])
                for ka in (0, 4)] for h in range(8)]
    U["g"] = [colu(win, C_GA), colu(win, C_GA + 512), colu(win, C_GH), colu(win, C_GH + 512)]
    U["p"] = [colu(W["w_proj_attn"], 0), colu(W["w_proj_attn"], 512),
              colu(W["w_proj_hgrn"], 0), colu(W["w_proj_hgrn"], 512)]
    U["o"] = [colu(W["w_out"], 0), colu(W["w_out"], 512)]
    return U


def piece_elems(pc):
    return (pc[2] - pc[1]) * sum(w for _, w in pc[3])


KIND_ORDER = ["f1a", "f1b", "tm", "hg", "g", "p", "o", "f2a", "f2b"]
TILE_SEQ = ([("f1a", g) for g in range(11)] + [("f1b", u) for u in range(6)] +
            [("tm", u) for u in range(5)] + [("hg", h) for h in range(8)] +
            [("g", 0), ("p", 0), ("g", 2), ("p", 2), ("g", 1), ("p", 1), ("g", 3), ("p", 3)] +
            [("o", 0), ("o", 1)] +
            [("f2a", g) for g in range(11)] + [("f2b", u) for u in range(6)])


def build(S):
    NT = S // T
    NB = S // 128
    nc = bass.Bass("TRN2", target_bir_lowering=False)
    x = nc.dram_tensor("x", [S, D], F32, kind="ExternalInput").ap()
    rope = nc.dram_tensor("rope_cs", [S, 128], F32, kind="ExternalInput").ap()
    W = {n: nc.dram_tensor(n, WSHAPES[n], F32, kind="ExternalInput").ap() for n in WNAMES}
    out = nc.dram_tensor("out", [S, D], F32, kind="ExternalOutput").ap()
    U = mk_units(W)
    UE = {k: [sum(piece_elems(pc) for pc in u) for u in us] for k, us in U.items()}
    scr = {k: nc.dram_tensor("scr_" + k, [len(U[k]), 128, max(UE[k])], BF16, kind="Internal").ap() for k in U}

    P = Prog()
    es = contextlib.ExitStack()
    with es:
        def sb(name, shape, dt):
            return es.enter_context(nc.sbuf_tensor(name, shape, dt))

        def sem(name):
            return es.enter_context(nc.semaphore(name))

        xa = sb("xa", [128, TB * D], F32)
        xa3 = xa[:].rearrange("p (j n) -> p j n", n=D)
        xT = sb("xT", [128, KC * T], BF16)
        xT3 = xT[:].rearrange("p (c t) -> p c t", t=T)
        gT = sb("gT", [128, FC * T], BF16)
        gT3 = gT[:].rearrange("p (c t) -> p c t", t=T)
        yaT3 = gT[:, 0:8 * T].rearrange("p (c t) -> p c t", t=T)
        yhT3 = gT[:, 8 * T:16 * T].rearrange("p (c t) -> p c t", t=T)
        stg16 = [gT[:, k * 2048:(k + 1) * 2048] for k in range(4)]
        ws = [sb("ws%d" % s, [128, 4096], BF16) for s in range(NS)]
        gbt = sb("gbt", [128, 2 * D], F32)
        gb3 = gbt[:].rearrange("p (a n) -> p a n", n=D)
        sl = [sb("sl%d" % k, [128, T], F32) for k in range(2)]
        st = [sb("st%d" % k, [128, 12], F32) for k in range(4)]
        mv = [sb("mv%d" % k, [128, 2], F32) for k in range(4)]
        sd = [sb("sd%d" % k, [128, 1], F32) for k in range(4)]
        rstd = [sb("rstd%d" % k, [128, 1], F32) for k in range(4)]
        nmr = [sb("nmr%d" % k, [128, 1], F32) for k in range(4)]
        ident = sb("ident", [128, 128], F32)
        identb = sb("identb", [128, 128], BF16)
        ones = sb("ones", [128, 128], BF16)
        onesm = sb("onesm", [128, 128], BF16)
        maskA = sb("maskA", [128, 256], BF16)
        mask2 = sb("mask2", [128, 128], BF16)
        rm = sb("rm", [128, T], F32)
        epsc = sb("epsc", [128, 3], F32)
        bfm = sb("bfm", [128, 40], F32)
        gcol = sb("gcol", [128, 16], F32)
        bcol = sb("bcol", [128, 16], F32)
        lbl = sb("lbl", [128, 16], F32)
        lb = sb("lb", [128, 8], F32)
        oml = sb("oml", [128, 8], F32)
        noml = sb("noml", [128, 8], F32)
        ng = sb("ng", [128, 1], F32)
        esk = sb("esk", [128, 8], F32)
        brow = sb("brow", [1, 2560], BF16)
        cst = sb("cst", [128, TB * 128], F32)
        cst3 = cst[:].rearrange("p (j c) -> p j c", c=128)
        qkb = sb("qkb", [128, TB * 1536], BF16)
        qkb3 = qkb[:].rearrange("p (j n) -> p j n", n=1536)
        qT = sb("qT", [128, 8 * T], BF16)
        qT3 = qT[:].rearrange("p (c t) -> p c t", t=T)
        kT = sb("kT", [128, 4 * 640], BF16)
        kT3 = kT[:].rearrange("p (g t) -> p g t", t=640)
        vt = sb("vt", [128, 5 * 256], BF16)
        vt3 = vt[:].rearrange("p (b n) -> p b n", n=256)
        iht = sb("iht", [128, TB * 1024], BF16)
        iht3 = iht[:].rearrange("p (j n) -> p j n", n=1024)
        rt = sb("rt", [128, 4 * 64], F32)
        rq = sb("rq", [128, 128], F32)
        rt4 = rt[:].rearrange("p (a h d) -> p a h d", h=8, d=8)
        pexp = [sb("pexp%d" % k, [128, 512], BF16) for k in range(2)]
        PT = [sb("PT%d" % k, [128, 512], BF16) for k in range(2)]
        rec = [sb("rec%d" % k, [128, 128], F32) for k in range(2)]
        tmp8a = sb("tmp8a", [128, 2048], F32)
        tmp8b = sb("tmp8b", [128, 2048], F32)
        stg32 = [tmp8a, tmp8b, xa[:, 0:2048], xa[:, 2048:4096]]
        hA, hB, hC, hD = (tmp8a[:, k * T:(k + 1) * T] for k in range(4))
        m1 = tmp8b
        xb = [tmp8a[:, 0:D], tmp8a[:, D:2 * D], tmp8b[:, 0:D], tmp8b[:, D:2 * D]]
        XB_ALIAS = [["hA", "hB"], ["hC", "hD"], [("m1", 0), ("m1", 1)], [("m1", 2), ("m1", 3)]]
        m13 = m1[:].rearrange("p (c t) -> p c t", t=T)
        dec = [sb("dec%d" % k, [128, 8], F32) for k in range(3)]
        qdec = [sb("qdec%d" % k, [128, T], BF16) for k in range(3)]
        kinv = [sb("kinv%d" % k, [128, T], BF16) for k in range(3)]
        kend = [sb("kend%d" % k, [128, T], BF16) for k in range(3)]
        qTf = qT[:].bitcast(F32)
        sog = [qTf[:, 0:T], qTf[:, T:2 * T], qTf[:, 3 * T:4 * T]]
        kendTM = [sb("kendTM%d" % k, [128, T], BF16) for k in range(2)]
        msk = [sb("msk%d" % k, [128, T], BF16) for k in range(2)]
        S16 = [sb("S16_%d" % k, [128, 8 * 128], BF16) for k in range(2)]
        state = sb("state", [128, 8 * 2 * 128], F32)
        state4 = state[:].rearrange("p (h s e) -> p h s e", s=2, e=128)
        osq = sb("osq", [128, T], BF16)
        rs = qTf[:, 2 * T:3 * T]
        sga = sb("sga", [128, 4 * T], BF16)
        sga3 = sga[:].rearrange("p (c t) -> p c t", t=T)
        sgh = sb("sgh", [128, 4 * T], BF16)
        sgh3 = sgh[:].rearrange("p (c t) -> p c t", t=T)
        m2 = sl
        mT = sb("mT", [128, 8 * T], BF16)
        mT3 = mT[:].rearrange("p (c t) -> p c t", t=T)
        ps = [es.enter_context(nc.psum_tensor("ps%d" % b, [128, 512], F32)) for b in range(8)]
        psb = [p_[:].bitcast(BF16) for p_ in ps]

        eng_sems = {e: sem("s_" + e) for e in ("pe", "act", "dve", "pool")}
        wsem = [sem("w%d" % s) for s in range(NS)]
        s32sem = [sem("s32_%d" % k) for k in range(4)]
        s16sem = [sem("s16_%d" % k) for k in range(4)]
        xsem = [sem("xl%d" % j) for j in range(TB)]
        osem = [sem("os%d" % j) for j in range(TB)]
        gsem = sem("gsem")
        csem = sem("csem")
        block = es.enter_context(nc.Block())

        def ps_r(b):
            return ("ps", b)

        def cdma(out_ap, in_ap, w, slow=False):
            if slow:
                P.op("pool", lambda e: e.dma_start(out=out_ap, in_=in_ap, allow_slow_non_contiguous=True),
                     writes=["cdma", w], dma_sem=csem)
            else:
                P.op("pool", lambda e: e.dma_start(out=out_ap, in_=in_ap), writes=["cdma", w], dma_sem=csem)

        b_in = W["b_in"]
        cdma(bfm[:, 0:16], b_in[0, C_FH:C_IH].rearrange("(c p) -> p c", p=128), "bfm", slow=True)
        cdma(bfm[:, 16:40], b_in[0, C_OG:7680].rearrange("(c p) -> p c", p=128), "bfm", slow=True)
        for li_, (gn_, bn_) in enumerate((("ln1_g", "ln1_b"), ("ln2_g", "ln2_b"))):
            cdma(gcol[:, li_ * 8:(li_ + 1) * 8], W[gn_][0, :].rearrange("(c p) -> p c", p=128), "gcol", slow=True)
            cdma(bcol[:, li_ * 8:(li_ + 1) * 8], W[bn_][0, :].rearrange("(c p) -> p c", p=128), "bcol", slow=True)
        cdma(lbl[:].rearrange("p (r h) -> p r h", h=8), W["hgrn_lb_logits"].rearrange("r (h p) -> p r h", p=128), "lbl", slow=True)
        cdma(ng[:], W["hgrn_norm_g"][0, :].rearrange("(p o) -> p o", o=1), "ng", slow=True)
        sk_t = W["attn_sinks"].tensor
        cdma(esk[0:64, :], bass.AP(sk_t, 0, [[0, 64], [2, 8]]), "esk", slow=True)
        cdma(esk[64:128, :], bass.AP(sk_t, 1, [[0, 64], [2, 8]]), "esk", slow=True)
        cdma(tmp8a[0:1, 0:1536], b_in[0:1, 0:1536], ("stg32", 0))
        cdma(tmp8b[0:1, 0:1024], b_in[0:1, C_IH:C_IH + 1024], ("stg32", 1))

        P.op("pool", lambda e: e.memset(ident[:], 1.0), writes=["ident"])
        P.op("pool", lambda e: e.affine_select(out=ident[:], in_=ident[:], pattern=[[-1, 128]], compare_op=ALU.is_equal,
                                                 fill=0.0, base=0, channel_multiplier=1), reads=["ident"], writes=["ident"])
        P.op("pool", lambda e: e.tensor_copy(out=identb[:], in_=ident[:]), reads=["ident"], writes=["identb"])
        P.op("pool", lambda e: e.memset(ones[:], 1.0), writes=["ones"])
        P.op("pool", lambda e: e.memset(onesm[:], 1.0 / 128.0), writes=["onesm"])
        P.op("pool", lambda e: e.memset(maskA[:], 1.0), writes=["maskA"])
        P.op("pool", lambda e: e.affine_select(out=maskA[:, 0:128], in_=maskA[:, 0:128], pattern=[[-1, 128]],
                                                 compare_op=ALU.is_ge, fill=0.0, base=-1, channel_multiplier=1),
             reads=["maskA"], writes=["maskA"])
        P.op("pool", lambda e: e.affine_select(out=maskA[:, 128:256], in_=maskA[:, 128:256], pattern=[[1, 128]],
                                                 compare_op=ALU.is_ge, fill=0.0, base=0, channel_multiplier=-1),
             reads=["maskA"], writes=["maskA"])
        P.op("pool", lambda e: e.memset(mask2[:], 1.0), writes=["mask2"])
        P.op("pool", lambda e: e.affine_select(out=mask2[:], in_=mask2[:], pattern=[[1, 128]], compare_op=ALU.is_ge,
                                                 fill=0.0, base=0, channel_multiplier=-1), reads=["mask2"], writes=["mask2"])
        P.op("pool", lambda e: e.memset(mask2[0:64, 64:128], 0.0), reads=["mask2"], writes=["mask2"])
        P.op("pool", lambda e: e.memset(rm[:], 1.0), writes=["rm"])
        P.op("pool", lambda e: e.memset(rm[:].rearrange("p (c t) -> p c t", t=64)[:, :, 0:1], 0.0), reads=["rm"], writes=["rm"])
        P.op("pool", lambda e: e.memset(epsc[:, 0:1], 4e-5), writes=["epsc"])
        P.op("pool", lambda e: e.memset(epsc[:, 1:2], 1e-5), reads=["epsc"], writes=["epsc"])
        P.op("pool", lambda e: e.memset(epsc[:, 2:3], 1e-6), reads=["epsc"], writes=["epsc"])
        P.op("pool", lambda e: e.memset(state[:], 0.0), writes=[("state", h_, s_) for h_ in range(8) for s_ in range(2)])
        P.op("dve", lambda e: e.tensor_tensor(out=lb[:], in0=lbl[:, 0:8], in1=lbl[:, 8:16], op=ALU.subtract),
             reads=["lbl"], writes=["lb"])
        P.op("act", lambda e: e.activation(out=lb[:], in_=lb[:], func=AF.Sigmoid), reads=["lb"], writes=["lb"])
        P.op("dve", lambda e: e.tensor_scalar(out=oml[:], in0=lb[:], scalar1=-1.0, scalar2=1.0, op0=ALU.mult, op1=ALU.add),
             reads=["lb"], writes=["oml"])
        P.op("dve", lambda e: e.tensor_scalar(out=noml[:], in0=lb[:], scalar1=-1.0, scalar2=None, op0=ALU.add),
             reads=["lb"], writes=["noml"])
        P.op("act", lambda e: e.activation(out=esk[:], in_=esk[:], func=AF.Exp), reads=["esk"], writes=["esk"])
        P.op("dve", lambda e: e.tensor_copy(out=brow[0:1, 0:1536], in_=tmp8a[0:1, 0:1536]), reads=[("stg32", 0)], writes=["brow"])
        P.op("dve", lambda e: e.tensor_copy(out=brow[0:1, 1536:2560], in_=tmp8b[0:1, 0:1024]), reads=[("stg32", 1), "brow"], writes=["brow"])
        CONST = ["ident", "identb", "ones", "onesm", "maskA", "mask2", "rm", "epsc", "bfm", "lb", "oml", "noml",
                 "ng", "esk", "brow", "cs"]
        for en in ("pe", "act", "dve", "pool"):
            pass

        cast_engs = ("act", "dve", "pool")
        pcount = [0]

        def convert_unit(kind, u):
            off = 0
            for pc in U[kind][u]:
                srcm, ka, kb, cols = pc
                nk = kb - ka
                Wd = sum(w for _, w in cols)
                n = nk * Wd
                k = pcount[0]
                pcount[0] += 1
                sbi = k % 4
                s32v = stg32[sbi][:, 0:n].rearrange("p (k w) -> p k w", w=Wd)
                srcv = srcm.rearrange("(kc p) f -> p kc f", p=128)
                o = 0
                for (c0, w) in cols:
                    P.op("sp", lambda e, o=o, w=w, c0=c0, s32v=s32v, srcv=srcv, ka=ka, kb=kb:
                         e.dma_start(out=s32v[:, :, o:o + w], in_=srcv[:, ka:kb, c0:c0 + w]),
                         writes=[("stg32", sbi)], dma_sem=s32sem[sbi])
                    o += w
                ce = cast_engs[k % 3]
                if ce == "act":
                    P.op("act", lambda e, sbi=sbi, n=n: e.activation(out=stg16[sbi][:, 0:n], in_=stg32[sbi][:, 0:n], func=AF.Copy),
                         reads=[("stg32", sbi)], writes=[("stg16", sbi)])
                else:
                    P.op(ce, lambda e, sbi=sbi, n=n: e.tensor_copy(out=stg16[sbi][:, 0:n], in_=stg32[sbi][:, 0:n]),
                         reads=[("stg32", sbi)], writes=[("stg16", sbi)])
                P.op("act", lambda e, sbi=sbi, n=n, off=off, kind=kind, u=u:
                     e.dma_start(out=scr[kind][u, :, off:off + n], in_=stg16[sbi][:, 0:n]),
                     reads=[("stg16", sbi)], writes=[("scr", kind, u)], dma_sem=s16sem[sbi])
                off += n

        for kind in KIND_ORDER:
            for u in range(len(U[kind])):
                convert_unit(kind, u)

        if DEBUG_STAGE == 1:
            seq_all = TILE_SEQ[:17] * NT
        elif DEBUG_STAGE == 2:
            seq_all = TILE_SEQ[:40] * NT
        elif DEBUG_STAGE in (21, 22, 23):
            seq_all = TILE_SEQ[:22] * NT
        elif DEBUG_STAGE == 24:
            seq_all = TILE_SEQ[:30] * NT
        else:
            seq_all = TILE_SEQ * NT

        class WL:
            next_emit = 0
            cur = -1

        def wl_emit():
            n = WL.next_emit
            if n >= len(seq_all):
                return
            kind, u = seq_all[n]
            s = n % NS
            E = UE[kind][u]
            P.op("sp", lambda e, s=s, E=E, kind=kind, u=u: e.dma_start(out=ws[s][:, 0:E], in_=scr[kind][u, :, 0:E]),
                 reads=[("scr", kind, u)], writes=[("ws", s)], dma_sem=wsem[s])
            WL.next_emit += 1

        def wl_acquire(kind, u):
            WL.cur += 1
            assert seq_all[WL.cur] == (kind, u), (seq_all[WL.cur], kind, u)
            while WL.next_emit <= WL.cur:
                wl_emit()
            return WL.cur % NS

        def wl_release():
            while WL.next_emit < min(WL.cur + 1 + NS, len(seq_all)):
                wl_emit()

        def load_x(i, j, extra_writes=()):
            r0 = i * T + j * 128
            P.op("pool", lambda e: e.dma_start(out=xb[j], in_=x[r0:r0 + 128, :]),
                 writes=[("xb", j)] + XB_ALIAS[j] + list(extra_writes), dma_sem=xsem[j])

        LN_GB = [("ln1_g", "ln1_b"), ("ln2_g", "ln2_b"), ("ln3_g", "ln3_b")]

        def load_gb(which):
            gname, bname = LN_GB[which]
            gt_ = W[gname].tensor
            bt_ = W[bname].tensor
            P.op("pool", lambda e: e.dma_start(out=gb3[:, 0, :], in_=bass.AP(gt_, 0, [[0, 128], [1, D]])),
                 writes=["gb"], dma_sem=gsem)
            P.op("pool", lambda e: e.dma_start(out=gb3[:, 1, :], in_=bass.AP(bt_, 0, [[0, 128], [1, D]])),
                 writes=["gb"], dma_sem=gsem)

        def mm(out_ap, lhsT, rhs, start, stop, reads, writes):
            P.op("pe", lambda e: e.matmul(out_ap, lhsT=lhsT, rhs=rhs, start=start, stop=stop), reads=reads, writes=writes)

        evac_rr = [0]

        def transposes_x(aff=None, from_xb=False):
            for c in range(KC):
                b = 4 + (c % 4)
                for j in range(TB):
                    if from_xb:
                        src_ = xb[j][:, c * 128:(c + 1) * 128]
                        rds_ = [("xb", j), "ident"] + XB_ALIAS[j]
                    else:
                        src_ = xa3[:, j, c * 128:(c + 1) * 128]
                        rds_ = [("xa", j), "ident"]
                    P.op("pe", lambda e, b=b, j=j, src_=src_: e.transpose(out=ps[b][:, j * 128:(j + 1) * 128], in_=src_, identity=ident[:]),
                         reads=rds_, writes=[ps_r(b)])
                if aff is None:
                    if c % 2 == 0:
                        P.op("act", lambda e, b=b, c=c: e.activation(out=xT3[:, c, :], in_=ps[b][:, :], func=AF.Copy),
                             reads=[ps_r(b)], writes=[("xT", c)])
                    else:
                        P.op("dve", lambda e, b=b, c=c: e.tensor_copy(out=xT3[:, c, :], in_=ps[b][:, :]),
                             reads=[ps_r(b)], writes=[("xT", c)])
                else:
                    gc_ = gcol[:, aff * 8 + c:aff * 8 + c + 1]
                    bc_ = bcol[:, aff * 8 + c:aff * 8 + c + 1]
                    if c % 2 == 0:
                        P.op("act", lambda e, b=b, c=c, gc_=gc_, bc_=bc_: e.activation(out=xT3[:, c, :], in_=ps[b][:, :], func=AF.Identity,
                                                                                   scale=gc_, bias=bc_),
                             reads=[ps_r(b), "gcol", "bcol"], writes=[("xT", c)])
                    else:
                        P.op("dve", lambda e, b=b, c=c, gc_=gc_, bc_=bc_: e.tensor_scalar(out=xT3[:, c, :], in0=ps[b][:, :], scalar1=gc_, scalar2=bc_,
                                                                                      op0=ALU.mult, op1=ALU.add),
                             reads=[ps_r(b), "gcol", "bcol"], writes=[("xT", c)])
            if aff is not None:
                for j in range(TB):
                    ln_gamma(j)
                for j in range(TB):
                    ln_beta(j)
                load_gb(aff + 1)

        def ln_stats(j, epsk):
            k = j
            P.op("dve", lambda e: e.bn_stats(out=st[k][:, 0:6], in_=xa3[:, j, 0:512]), reads=[("xa", j)], writes=[("st", k)])
            P.op("dve", lambda e: e.bn_stats(out=st[k][:, 6:12], in_=xa3[:, j, 512:1024]), reads=[("xa", j), ("st", k)],
                 writes=[("st", k)])
            P.op("dve", lambda e: e.bn_aggr(out=mv[k][:, 0:2], in_=st[k][:, 0:12]), reads=[("st", k)], writes=[("mv", k)])
            P.op("act", lambda e: e.activation(out=sd[k][:, 0:1], in_=mv[k][:, 1:2], func=AF.Sqrt, bias=epsc[:, epsk:epsk + 1], scale=1.0),
                 reads=[("mv", k), "epsc"], writes=[("sd", k)])
            P.op("dve", lambda e: e.reciprocal(out=rstd[k][:, 0:1], in_=sd[k][:, 0:1]), reads=[("sd", k)], writes=[("rstd", k)])
            P.op("dve", lambda e: e.scalar_tensor_tensor(out=nmr[k][:, 0:1], in0=mv[k][:, 0:1], scalar=-1.0, in1=rstd[k][:, 0:1],
                                                         op0=ALU.mult, op1=ALU.mult),
                 reads=[("mv", k), ("rstd", k)], writes=[("nmr", k)])

        def ln_norm(j):
            k = j
            P.op("act", lambda e: e.activation(out=xa3[:, j, :], in_=xa3[:, j, :], func=AF.Identity,
                                               scale=rstd[k][:, 0:1], bias=nmr[k][:, 0:1]),
                 reads=[("xa", j), ("rstd", k), ("nmr", k)], writes=[("xa", j)])

        def ln_gamma(j):
            P.op("pool", lambda e: e.tensor_tensor(out=xa3[:, j, :], in0=xa3[:, j, :], in1=gb3[:, 0, :], op=ALU.mult),
                 reads=[("xa", j), "gb"], writes=[("xa", j)])

        def ln_beta(j):
            P.op("dve", lambda e: e.tensor_tensor(out=xa3[:, j, :], in0=xa3[:, j, :], in1=gb3[:, 1, :], op=ALU.add),
                 reads=[("xa", j), "gb"], writes=[("xa", j)])

        def layernorm_all(epsk, after=None, defer_affine=False):
            for j in range(TB):
                ln_stats(j, epsk)
            for j in range(TB):
                ln_norm(j)
            if defer_affine:
                return
            for j in range(TB):
                ln_gamma(j)
            for j in range(TB):
                ln_beta(j)
                if after is not None:
                    after(j)

        def ffn(i, which):
            ka_, kb_ = ("f1a", "f1b") if which == 1 else ("f2a", "f2b")
            if which == 1:
                transposes_x(None, from_xb=True)
            else:
                transposes_x(1)
                if i + 1 < NT:
                    for j_ in range(TB):
                        load_x(i + 1, j_)
            for g in range(11):
                s = wl_acquire(ka_, g)
                wv = ws[s][:, :].rearrange("p (m k f) -> p m k f", m=2, k=8)
                for q in range(2):
                    fc = 2 * g + q
                    b1 = 2 * (fc % 2)
                    b3 = b1 + 1
                    for m, b in ((0, b1), (1, b3)):
                        for kc in range(KC):
                            mm(ps[b][:, :], wv[:, m, kc, q * 128:(q + 1) * 128], xT3[:, kc, :], kc == 0, kc == KC - 1,
                               [("ws", s), ("xT", kc)], [ps_r(b)])
                    kk = fc % 2
                    P.op("act", lambda e, b1=b1, kk=kk: e.activation(out=sl[kk][:, :], in_=ps[b1][:, :], func=AF.Silu),
                         reads=[ps_r(b1)], writes=[("sl", kk)])
                    P.op("dve", lambda e, b3=b3, kk=kk, fc=fc: e.tensor_tensor(out=gT3[:, fc, :], in0=sl[kk][:, :], in1=ps[b3][:, :], op=ALU.mult),
                         reads=[("sl", kk), ps_r(b3)], writes=[("gT", fc)])
                wl_release()
            for hf in range(2):
                banks = [4, 5, 6, 7] if hf == 0 else [0, 1, 2, 3]
                for ui, (k0, k1) in enumerate(((0, 8), (8, 16), (16, 22))):
                    s = wl_acquire(kb_, hf * 3 + ui)
                    wv = ws[s][:, :].rearrange("p (k f) -> p k f", f=512)
                    for kc in range(k0, k1):
                        for j in range(TB):
                            mm(ps[banks[j]][:, :], gT3[:, kc, j * 128:(j + 1) * 128], wv[:, kc - k0, :], kc == 0, kc == FC - 1,
                               [("ws", s), ("gT", kc)], [ps_r(banks[j])])
                    wl_release()
                for j in range(TB):
                    bj = banks[j]
                    if which == 1:
                        rsrc = xb[j][:, hf * 512:(hf + 1) * 512]
                        rrd = [("xb", j), ("xa", j), ps_r(bj)] + XB_ALIAS[j]
                    else:
                        rsrc = xa3[:, j, hf * 512:(hf + 1) * 512]
                        rrd = [("xa", j), ps_r(bj)]
                    P.op("dve", lambda e, j=j, bj=bj, hf=hf, rsrc=rsrc: e.scalar_tensor_tensor(
                        out=xa3[:, j, hf * 512:(hf + 1) * 512], in0=rsrc, scalar=2.0 * ALPHA,
                        in1=ps[bj][:, :], op0=ALU.mult, op1=ALU.add),
                        reads=rrd, writes=[("xa", j)])
            def after_ln(j):
                if which == 2:
                    r0 = i * T + j * 128
                    P.op("pool", lambda e, j=j, r0=r0: e.dma_start(out=out[r0:r0 + 128, :], in_=xa3[:, j, :]),
                         reads=[("xa", j)], dma_sem=osem[j])
            if which == 1:
                layernorm_all(0, None, defer_affine=True)
            else:
                layernorm_all(0, after_ln)
                load_gb(0)

        def rope_ops(b, ncols_heads, gbk, out_views):
            nh = ncols_heads
            psv = ps[b][:, 0:nh * 64].rearrange("p (h d) -> p h d", d=64)
            rqv = rq[:, 0:nh * 16].rearrange("p (h d) -> p h d", d=16)
            P.op("act", lambda e: e.activation(out=rqv, in_=psv[:, :, 0:16], func=AF.Copy),
                 reads=[ps_r(b)], writes=["rq"])
            t1 = rqv[:, :, 0:8]
            t2 = rqv[:, :, 8:16]
            jj_ = gbk % TB
            cosb = cst3[:, jj_, 0:64].rearrange("p (h d) -> p h d", d=8)[:, 0:nh, :]
            sinb = cst3[:, jj_, 64:128].rearrange("p (h d) -> p h d", d=8)[:, 0:nh, :]
            for a, (tt, cc) in enumerate(((t1, cosb), (t2, sinb), (t2, cosb), (t1, sinb))):
                P.op("dve", lambda e, a=a, tt=tt, cc=cc: e.tensor_tensor(out=rt4[:, a, 0:nh, :], in0=tt, in1=cc, op=ALU.mult),
                     reads=["rq", "cst", ("rt", a)], writes=[("rt", a)])
            return nh

        def mixer(i):
            xsem_ = csem
            for j_ in range(TB):
                r0_ = i * T + j_ * 128
                P.op("pool", lambda e, j_=j_, r0_=r0_: e.dma_start(out=cst3[:, j_, :], in_=rope[r0_:r0_ + 128, :]),
                     writes=["cst", "cdma"], dma_sem=csem)
            transposes_x(0)
            for u in range(5):
                s = wl_acquire("tm", u)
                wv = ws[s][:, :].rearrange("p (k f) -> p k f", f=512)
                for j in range(TB):
                    b = (u * TB + j) % 2
                    gbk = i * TB + j
                    for kc in range(KC):
                        mm(ps[b][:, :], xT3[:, kc, j * 128:(j + 1) * 128], wv[:, kc, :], kc == 0, ("NOK1" in os.environ and kc == KC - 1),
                           [("ws", s), ("xT", kc)], [ps_r(b)])
                    if "NOK1" not in os.environ:
                        mm(ps[b][:, :], ones[0:1, 0:128], brow[0:1, u * 512:(u + 1) * 512], False, True,
                           ["ones", "brow"], [ps_r(b)])
                    if u < 2:
                        P.op("act", lambda e, b=b, j=j, u=u: e.activation(out=qkb3[:, j, u * 512:(u + 1) * 512], in_=ps[b][:, :], func=AF.Copy),
                             reads=[ps_r(b)], writes=[("qkb", j)])
                        rope_ops(b, 8, gbk, None)
                        qv = qkb3[:, j, 0:1024].rearrange("p (h d) -> p h d", d=64)
                        P.op("dve", lambda e, qv=qv, u=u: e.tensor_tensor(out=qv[:, 8 * u:8 * u + 8, 0:8], in0=rt4[:, 0, :, :], in1=rt4[:, 1, :, :], op=ALU.subtract),
                             reads=[("rt", 0), ("rt", 1)], writes=[("qkb", j)])
                        P.op("dve", lambda e, qv=qv, u=u: e.tensor_tensor(out=qv[:, 8 * u:8 * u + 8, 8:16], in0=rt4[:, 2, :, :], in1=rt4[:, 3, :, :], op=ALU.add),
                             reads=[("rt", 2), ("rt", 3)], writes=[("qkb", j)])
                    elif u == 2:
                        kd = qkb3[:, j, 1024:1536].rearrange("p (h r d) -> p h r d", r=2, d=64)
                        kin3 = ps[b][:, 0:256].rearrange("p (h d) -> p h d", d=64)
                        P.op("act", lambda e, kd=kd, kin3=kin3: e.activation(out=kd[:, :, 0, :], in_=kin3, func=AF.Copy),
                             reads=[ps_r(b)], writes=[("qkb", j)])
                        P.op("dve", lambda e, kd=kd, kin3=kin3: e.tensor_copy(out=kd[:, :, 1, :], in_=kin3),
                             reads=[ps_r(b)], writes=[("qkb", j)])
                        P.op("dve", lambda e, b=b, j=j: e.tensor_copy(out=vt3[:, j + 1, :], in_=ps[b][:, 256:512]),
                             reads=[ps_r(b)], writes=[("vt", j + 1)])
                        rope_ops(b, 4, gbk, None)
                        for r_ in range(2):
                            P.op("dve", lambda e, kd=kd, r_=r_: e.tensor_tensor(
                                out=kd[:, :, r_, 0:8], in0=rt4[:, 0, 0:4, :], in1=rt4[:, 1, 0:4, :], op=ALU.subtract),
                                reads=[("rt", 0), ("rt", 1)], writes=[("qkb", j)])
                            P.op("dve", lambda e, kd=kd, r_=r_: e.tensor_tensor(
                                out=kd[:, :, r_, 8:16], in0=rt4[:, 2, 0:4, :], in1=rt4[:, 3, 0:4, :], op=ALU.add),
                                reads=[("rt", 2), ("rt", 3)], writes=[("qkb", j)])
                    else:
                        hcol = (u - 3) * 512
                        if j % 2 == 0:
                            P.op("act", lambda e, b=b, j=j, hcol=hcol: e.activation(out=iht3[:, j, hcol:hcol + 512], in_=ps[b][:, :], func=AF.Copy),
                                 reads=[ps_r(b)], writes=[("iht", j)])
                        else:
                            P.op("dve", lambda e, b=b, j=j, hcol=hcol: e.tensor_copy(out=iht3[:, j, hcol:hcol + 512], in_=ps[b][:, :]),
                                 reads=[ps_r(b)], writes=[("iht", j)])
                wl_release()
            if DEBUG_STAGE == 21:
                return
            for j in range(TB):
                for c in range(8):
                    P.op("pe", lambda e, j=j, c=c: e.transpose(out=psb[2][:, c * 128:(c + 1) * 128], in_=qkb3[:, j, c * 128:(c + 1) * 128], identity=identb[:]),
                         reads=[("qkb", j), "identb"], writes=[ps_r(2)])
                P.op("act", lambda e, j=j: e.activation(out=qT3[:, :, j * 128:(j + 1) * 128],
                                                        in_=psb[2][:, 0:1024].rearrange("p (c t) -> p c t", t=128), func=AF.Copy),
                     reads=[ps_r(2)], writes=[("qT", j)])
                for g in range(4):
                    P.op("pe", lambda e, j=j, g=g: e.transpose(out=psb[3][:, g * 128:(g + 1) * 128],
                                                               in_=qkb3[:, j, 1024 + g * 128:1024 + (g + 1) * 128], identity=identb[:]),
                         reads=[("qkb", j), "identb"], writes=[ps_r(3)])
                P.op("dve", lambda e, j=j: e.tensor_copy(out=kT3[:, :, (j + 1) * 128:(j + 2) * 128],
                                                         in_=psb[3][:, 0:512].rearrange("p (g t) -> p g t", t=128)),
                     reads=[ps_r(3)], writes=[("kT", j + 1)])
            if DEBUG_STAGE == 22:
                return
            pairs = [(j_, c_) for j_ in range(TB) for c_ in range(8)]

            def att_stage1(k):
                j, c = pairs[k]
                first = (i == 0 and j == 0)
                kbs = [1] if first else [0, 1]
                par = k % 2
                g = c // 2
                sbanks = (par, 4 + par)
                for hd in range(2):
                    pb = 64 * hd
                    sbk = sbanks[hd]
                    for kb in kbs:
                        mm(ps[sbk][:, kb * 128:(kb + 1) * 128],
                           kT3[pb:pb + 64, g, (j + kb) * 128:(j + kb + 1) * 128], qT3[pb:pb + 64, c, j * 128:(j + 1) * 128],
                           True, True, [("kT", j + kb), ("qT", j)], [ps_r(sbk)])
                    c0_ = 128 if first else 0
                    P.op("act", lambda e, sbk=sbk, hd=hd, c0_=c0_, par=par: e.activation(
                        out=pexp[par][:, hd * 256 + c0_:(hd + 1) * 256], in_=ps[sbk][:, c0_:256], func=AF.Exp, scale=0.125),
                        reads=[ps_r(sbk)], writes=[("pexp", par)])
                if first:
                    pev = pexp[par][:, :].rearrange("p (h n) -> p h n", n=256)[:, :, 128:256]
                    ptv = PT[par][:, :].rearrange("p (h n) -> p h n", n=256)[:, :, 128:256]
                    mk = maskA[:, 128:256].unsqueeze(1).broadcast_to([128, 2, 128])
                else:
                    pev = pexp[par][:, :].rearrange("p (h n) -> p h n", n=256)
                    ptv = PT[par][:, :].rearrange("p (h n) -> p h n", n=256)
                    mk = maskA[:, 0:256].unsqueeze(1).broadcast_to([128, 2, 256])
                P.op("dve", lambda e, ptv=ptv, pev=pev, mk=mk: e.tensor_tensor(out=ptv, in0=pev, in1=mk, op=ALU.mult),
                     reads=[("pexp", par), "maskA"], writes=[("PT", par)])

            def att_stage2(k):
                j, c = pairs[k]
                first = (i == 0 and j == 0)
                kbs = [1] if first else [0, 1]
                par = k % 2
                ob = 6 + par
                g = c // 2
                for which_ in range(2):
                    for hd in range(2):
                        pb = 64 * hd
                        for idx, kb in enumerate(kbs):
                            lhs = vt3[:, j + kb, g * 64:(g + 1) * 64] if which_ == 0 else ones[:, 0:64]
                            rd = [("vt", j + kb), ("PT", par)] if which_ == 0 else ["ones", ("PT", par)]
                            mm(ps[ob][pb:pb + 64, which_ * 128:(which_ + 1) * 128], lhs,
                               PT[par][:, hd * 256 + kb * 128: hd * 256 + (kb + 1) * 128],
                               idx == 0, idx == len(kbs) - 1, rd, [ps_r(ob)])
                P.op("act", lambda e: e.activation(out=rec[par][:, :], in_=ps[ob][:, 128:256], func=AF.Ln, bias=esk[:, c:c + 1], scale=1.0),
                     reads=[ps_r(ob), "esk"], writes=[("rec", par)])
                P.op("act", lambda e: e.activation(out=rec[par][:, :], in_=rec[par][:, :], func=AF.Exp, scale=-1.0),
                     reads=[("rec", par)], writes=[("rec", par)])
                P.op("dve", lambda e: e.tensor_tensor(out=yaT3[:, c, j * 128:(j + 1) * 128], in0=ps[ob][:, 0:128],
                                                      in1=rec[par][:, :], op=ALU.mult),
                     reads=[ps_r(ob), ("rec", par)], writes=[("yaT", c, j)])

            att_stage1(0)
            for k_ in range(len(pairs)):
                if k_ + 1 < len(pairs):
                    att_stage1(k_ + 1)
                att_stage2(k_)
            P.op("pool", lambda e: e.tensor_copy(out=kT3[:, :, 0:128], in_=kT3[:, :, 512:640]), reads=[("kT", 4)], writes=[("kT", 0)])
            P.op("pool", lambda e: e.tensor_copy(out=vt3[:, 0, :], in_=vt3[:, 4, :]), reads=[("vt", 4)], writes=[("vt", 0)])

            if DEBUG_STAGE == 23:
                return
            def Pstage(h, mid_hook=None):
                s = wl_acquire("hg", h)
                wv = ws[s][:, 0:8 * 384].rearrange("p (k f) -> p k f", f=384)
                for m in range(3):
                    for kc in range(KC):
                        mm(ps[m][:, :], wv[:, kc, m * 128:(m + 1) * 128], xT3[:, kc, :], kc == 0, kc == KC - 1,
                           [("ws", s), ("xT", kc)], [ps_r(m)])
                    if m == 1 and mid_hook is not None:
                        mid_hook()
                wl_release()

            def Estage(h):
                par = h % 3
                P.op("act", lambda e: e.activation(out=hA, in_=ps[0][:, :], func=AF.Sigmoid, bias=bfm[:, h:h + 1], scale=1.0),
                     reads=[ps_r(0), "bfm"], writes=["hA"])
                P.op("act", lambda e: e.activation(out=hD, in_=ps[1][:, :], func=AF.Silu, bias=bfm[:, 8 + h:9 + h], scale=1.0),
                     reads=[ps_r(1), "bfm"], writes=["hD"])
                P.op("act", lambda e: e.activation(out=sog[par][:, :], in_=ps[2][:, :], func=AF.Silu, bias=bfm[:, 16 + h:17 + h], scale=1.0),
                     reads=[ps_r(2), "bfm"], writes=[("sog", par)])
                P.op("act", lambda e: e.activation(out=hB, in_=hA, func=AF.Ln, bias=lb[:, h:h + 1], scale=oml[:, h:h + 1]),
                     reads=["hA", "lb", "oml"], writes=["hB"])
                P.op("dve", lambda e: e.tensor_scalar(out=hA, in0=hA, scalar1=noml[:, h:h + 1], scalar2=oml[:, h:h + 1], op0=ALU.mult, op1=ALU.add),
                     reads=["hA", "noml", "oml"], writes=["hA"])
                P.op("dve", lambda e: e.tensor_tensor_scan(out=hC, data0=rm[:, :], data1=hB, initial=0.0, op0=ALU.mult, op1=ALU.add),
                     reads=["rm", "hB"], writes=["hC"])
                P.op("act", lambda e: e.activation(out=dec[par][:, :], in_=hC.rearrange("p (c t) -> p c t", t=64)[:, :, 63], func=AF.Exp),
                     reads=["hC"], writes=[("dec", par)])
                P.op("act", lambda e: e.activation(out=hB, in_=hC, func=AF.Exp), reads=["hC"], writes=["hB"])
                P.op("act", lambda e: e.activation(out=hC, in_=hC, func=AF.Exp, scale=-1.0), reads=["hC"], writes=["hC"])
                P.op("dve", lambda e: e.tensor_tensor(out=qdec[par][:, :], in0=hD, in1=hB, op=ALU.mult),
                     reads=["hD", "hB"], writes=[("qdec", par)])
                P.op("dve", lambda e: e.tensor_tensor(out=hA, in0=hA, in1=hC, op=ALU.mult), reads=["hA", "hC"], writes=["hA"])
                P.op("dve", lambda e: e.tensor_copy(out=kinv[par][:, :], in_=hA), reads=["hA"], writes=[("kinv", par)])
                P.op("dve", lambda e: e.tensor_tensor(out=kend[par][:, :].rearrange("p (c t) -> p c t", t=64),
                                                      in0=hA.rearrange("p (c t) -> p c t", t=64),
                                                      in1=dec[par][:, :].unsqueeze(2).broadcast_to([128, 8, 64]), op=ALU.mult),
                     reads=["hA", ("dec", par)], writes=[("kend", par)])

            def Xa_stage(h):
                par = h % 2
                p3 = h % 3
                for j in range(TB):
                    P.op("pe", lambda e, j=j: e.transpose(out=psb[3][:, j * 128:(j + 1) * 128], in_=kend[p3][:, j * 128:(j + 1) * 128], identity=identb[:]),
                         reads=[("kend", p3), "identb"], writes=[ps_r(3)])
                P.op("act", lambda e: e.activation(out=kendTM[par][:, :], in_=psb[3][:, 0:512], func=AF.Copy),
                     reads=[ps_r(3)], writes=[("kendTM", par)])
                for j in range(TB):
                    mm(ps[4][:, j * 128:(j + 1) * 128], kinv[p3][:, j * 128:(j + 1) * 128], qdec[p3][:, j * 128:(j + 1) * 128], True, True,
                       [("kinv", p3), ("qdec", p3)], [ps_r(4)])
                P.op("dve", lambda e: e.tensor_tensor(out=msk[par][:, :].rearrange("p (j t) -> p j t", t=128),
                                                      in0=ps[4][:, :].rearrange("p (j t) -> p j t", t=128),
                                                      in1=mask2[:, :].unsqueeze(1).broadcast_to([128, 4, 128]), op=ALU.mult),
                     reads=[ps_r(4), "mask2"], writes=[("msk", par)])

            def Xb_stage(h):
                par = h % 2
                p3 = h % 3
                P.op("dve", lambda e: e.tensor_copy(out=S16[par][:, 0:128], in_=state4[:, h, 0, :]),
                     reads=[("state", h, 0)], writes=[("S16", par, 0)])
                ktm = kendTM[par][:, :].rearrange("p (j d) -> p j d", d=128)
                for half in range(2):
                    for cc in range(4 * half, 4 * half + 4):
                        jj = cc // 2
                        p0 = 64 * (cc % 2)
                        ub = 5 if cc % 2 == 0 else 4
                        uc = (cc // 2) * 128
                        mm(ps[ub][:, uc:uc + 128], ktm[p0:p0 + 64, jj, :], iht3[p0:p0 + 64, jj, h * 128:(h + 1) * 128], True, True,
                           [("kendTM", par), ("iht", jj)], [ps_r(ub)])
                    for cc in range(4 * half, 4 * half + 4):
                        si = cc % 2
                        so = (cc + 1) % 2
                        ub = 5 if cc % 2 == 0 else 4
                        uc = (cc // 2) * 128
                        P.op("dve", lambda e, cc=cc, si=si, so=so, ub=ub, uc=uc: e.scalar_tensor_tensor(
                            out=state4[:, h, so, :], in0=state4[:, h, si, :], scalar=dec[p3][:, cc:cc + 1],
                            in1=ps[ub][:, uc:uc + 128], op0=ALU.mult, op1=ALU.add),
                            reads=[("state", h, si), ("dec", p3), ps_r(ub)], writes=[("state", h, so)])
                        if cc < 7:
                            P.op("dve", lambda e, cc=cc, so=so: e.tensor_copy(out=S16[par][:, (cc + 1) * 128:(cc + 2) * 128], in_=state4[:, h, so, :]),
                                 reads=[("state", h, so)], writes=[("S16", par, cc + 1)])

            def Oa_stage(h):
                par = h % 2
                p3 = h % 3
                for j in range(TB):
                    mm(ps[6][:, j * 128:(j + 1) * 128], iht3[:, j, h * 128:(h + 1) * 128], msk[par][:, j * 128:(j + 1) * 128], True, False,
                       [("iht", j), ("msk", par)], [ps_r(6)])
                    for q in range(2):
                        cc = 2 * j + q
                        mm(ps[6][:, j * 128 + q * 64: j * 128 + (q + 1) * 64], S16[par][:, cc * 128:(cc + 1) * 128],
                           qdec[p3][:, j * 128 + q * 64: j * 128 + (q + 1) * 64], False, q == 1,
                           [("S16", par, cc), ("qdec", p3)], [ps_r(6)])
                P.op("act", lambda e: e.activation(out=osq[:, :], in_=ps[6][:, :], func=AF.Square), reads=[ps_r(6)], writes=["osq"])

            def Ob_stage(h):
                par = h % 2
                p3 = h % 3
                mm(ps[7][:, :], onesm[:, :], osq[:, :], True, True, ["onesm", "osq"], [ps_r(7)])
                P.op("act", lambda e: e.activation(out=rs[:, :], in_=ps[7][:, :], func=AF.Ln, bias=epsc[:, 2:3], scale=1.0),
                     reads=[ps_r(7), "epsc"], writes=["rs"])
                P.op("act", lambda e: e.activation(out=rs[:, :], in_=rs[:, :], func=AF.Exp, scale=-0.5), reads=["rs"], writes=["rs"])
                P.op("dve", lambda e: e.scalar_tensor_tensor(out=rs[:, :], in0=ps[6][:, :], scalar=ng[:, 0:1], in1=rs[:, :], op0=ALU.mult, op1=ALU.mult),
                     reads=[ps_r(6), "ng", "rs"], writes=["rs"])
                P.op("pool", lambda e: e.tensor_tensor(out=yhT3[:, h, :], in0=rs[:, :], in1=sog[p3][:, :], op=ALU.mult),
                     reads=["rs", ("sog", p3)], writes=[("yhT", h)])

            for step in range(11):
                ho = step - 3
                hx = step - 2
                if 0 <= ho < 8:
                    Oa_stage(ho)
                if 0 <= hx < 8:
                    Xa_stage(hx)

                def mid(ho=ho, hx=hx):
                    if 0 <= ho < 8:
                        Ob_stage(ho)
                    if 0 <= hx < 8:
                        Xb_stage(hx)
                if step < 8:
                    Pstage(step, mid)
                    Estage(step)
                else:
                    mid()

            if DEBUG_STAGE == 24:
                return
            bankset = [0]

            def next_banks():
                bs = [0, 1, 2, 3] if bankset[0] % 2 == 0 else [4, 5, 6, 7]
                bankset[0] += 1
                return bs

            for gq in range(2):
                for (kind, u, role) in (("g", gq, "ga"), ("p", gq, "pa"), ("g", 2 + gq, "gh"), ("p", 2 + gq, "ph")):
                    s = wl_acquire(kind, u)
                    wv = ws[s][:, :].rearrange("p (k f) -> p k f", f=512)
                    bs = next_banks()
                    for cc in range(4):
                        c = 4 * gq + cc
                        b = bs[cc]
                        for kc in range(KC):
                            if role in ("ga", "gh"):
                                rhs, rr = xT3[:, kc, :], ("xT", kc)
                                rds = [("ws", s), rr]
                            elif role == "pa":
                                rhs = yaT3[:, kc, :]
                                rds = [("ws", s)] + [("yaT", kc, jj) for jj in range(TB)]
                            else:
                                rhs = yhT3[:, kc, :]
                                rds = [("ws", s), ("yhT", kc)]
                            mm(ps[b][:, :], wv[:, kc, cc * 128:(cc + 1) * 128], rhs, kc == 0, kc == KC - 1, rds, [ps_r(b)])
                        if role == "ga":
                            P.op("act", lambda e, b=b, cc=cc, c=c: e.activation(out=sga3[:, cc, :], in_=ps[b][:, :], func=AF.Sigmoid, bias=bfm[:, 24 + c:25 + c], scale=1.0),
                                 reads=[ps_r(b), "bfm"], writes=[("sga", cc)])
                        elif role == "gh":
                            P.op("act", lambda e, b=b, cc=cc, c=c: e.activation(out=sgh3[:, cc, :], in_=ps[b][:, :], func=AF.Sigmoid, bias=bfm[:, 32 + c:33 + c], scale=1.0),
                                 reads=[ps_r(b), "bfm"], writes=[("sgh", cc)])
                        elif role == "pa":
                            P.op("dve", lambda e, b=b, cc=cc: e.tensor_tensor(out=m13[:, cc, :], in0=ps[b][:, :], in1=sga3[:, cc, :], op=ALU.mult),
                                 reads=[ps_r(b), ("sga", cc)], writes=[("m1", cc)])
                        else:
                            kk = cc % 2
                            P.op("dve", lambda e, b=b, cc=cc, kk=kk: e.tensor_tensor(out=m2[kk][:, :], in0=ps[b][:, :], in1=sgh3[:, cc, :], op=ALU.mult),
                                 reads=[ps_r(b), ("sgh", cc)], writes=[("sl", kk)])
                            P.op("pool", lambda e, cc=cc, kk=kk, c=c: e.tensor_tensor(out=mT3[:, c, :], in0=m13[:, cc, :], in1=m2[kk][:, :], op=ALU.add),
                                 reads=[("m1", cc), ("sl", kk)], writes=[("mT", c)])
                    wl_release()
            for hf in range(2):
                s = wl_acquire("o", hf)
                wv = ws[s][:, :].rearrange("p (k f) -> p k f", f=512)
                bs = next_banks()
                for j in range(TB):
                    for kc in range(KC):
                        mm(ps[bs[j]][:, :], mT3[:, kc, j * 128:(j + 1) * 128], wv[:, kc, :], kc == 0, kc == KC - 1,
                           [("ws", s), ("mT", kc)], [ps_r(bs[j])])
                wl_release()
                for j in range(TB):
                    bj = bs[j]
                    P.op("dve", lambda e, j=j, bj=bj, hf=hf: e.scalar_tensor_tensor(
                        out=xa3[:, j, hf * 512:(hf + 1) * 512], in0=xa3[:, j, hf * 512:(hf + 1) * 512], scalar=ALPHA,
                        in1=ps[bj][:, :], op0=ALU.mult, op1=ALU.add),
                        reads=[("xa", j), ps_r(bj)], writes=[("xa", j)])
            layernorm_all(1, None, defer_affine=True)

        for n_ in range(NS):
            wl_emit()
        fence = [("stg32", k_) for k_ in range(4)] + [("stg16", k_) for k_ in range(4)]
        for j in range(TB):
            load_x(0, j, extra_writes=fence)
        load_gb(0)
        def dbg_store(i):
            for j in range(TB):
                r0 = i * T + j * 128
                P.op("pool", lambda e, j=j, r0=r0: e.dma_start(out=out[r0:r0 + 128, :], in_=xa3[:, j, :]),
                     reads=[("xa", j)], dma_sem=osem[j])
                if i + 1 < NT:
                    load_x(i + 1, j)

        for i in range(NT):
            ffn(i, 1)
            if DEBUG_STAGE == 1:
                dbg_store(i)
                continue
            mixer(i)
            if DEBUG_STAGE >= 2:
                dbg_store(i)
                continue
            ffn(i, 2)
        for j in range(TB):
            P.op("pool", lambda e, j=j: e.wait_ge(osem[j], 16 * NT))

        P.finalize(eng_sems)

        @block.sync
        def _(e):
            P.emit("sp", e)

        @block.tensor
        def _(e):
            P.emit("pe", e)

        @block.scalar
        def _(e):
            P.emit("act", e)

        @block.vector
        def _(e):
            P.emit("dve", e)

        @block.gpsimd
        def _(e):
            P.emit("pool", e)
    return nc


def rope_table(S):
    pos = np.arange(S, dtype=np.float32)
    inv = (np.float32(500000.0) ** (-(np.arange(0, 16, 2, dtype=np.float32)) / np.float32(16))).astype(np.float32)
    ang = (pos[:, None] * inv[None, :]).astype(np.float32)
    c = np.tile(np.cos(ang).astype(np.float32), (1, 8))
    s_ = np.tile(np.sin(ang).astype(np.float32), (1, 8))
    return np.ascontiguousarray(np.concatenate([c, s_], axis=1).astype(np.float32))


_CACHE = {}


def run(inputs, S, n_cores):
    if S not in _CACHE:
        _CACHE[S] = build(S)
    nc = _CACHE[S]
    x = np.asarray(inputs["x"], dtype=np.float32)
    wmap = {}
    for n in WNAMES:
        a = np.asarray(inputs[n], dtype=np.float32)
        if n != "hgrn_lb_logits":
            a = a.reshape(WSHAPES[n])
        wmap[n] = np.ascontiguousarray(a)
    rt_ = rope_table(S)
    in_maps = []
    for c in range(n_cores):
        m = {"x": np.ascontiguousarray(x[c]), "rope_cs": rt_}
        m.update(wmap)
        in_maps.append(m)
    res = run_bass_kernel_spmd(nc, in_maps, core_ids=list(range(n_cores)))
    return np.stack([np.asarray(r["out"]) for r in res.results], axis=0)


def kernel(**inputs):
    x = inputs["x"]
    B, S, _ = x.shape
    return run(inputs, S, B).astype(np.float32)
```
